# Optimizing a Trainium2 kernel written in Bass

```python
import math
import jax, jax.numpy as jnp
from jax import lax
import numpy as np

D_MODEL = 2048
BATCH = 8
SEQ = 2048
DEPTH = 1

W_A = D_MODEL // 2
DH_A = 128
H_A = W_A // DH_A
D_C = D_MODEL // 8
H_IDX = 16
D_IDX = 64
TOPK_MAX = 256
Q_BLOCK = 128
W_M = D_MODEL - W_A
H_M = 4
DV_M = W_M // H_M
DK_M = DV_M // 2
CONV_W = 4
CHUNK = 64
N_BUCKETS = 32
MAX_DIST = 128
ALPHA = (2 * DEPTH) ** 0.25
BETA = (8 * DEPTH) ** -0.25
LN_EPS = 1e-5

SPLITS = (
    W_A,
    D_C,
    W_A,
    H_IDX * D_IDX,
    D_IDX,
    H_IDX,
    H_M * DK_M,
    H_M * DK_M,
    W_M,
    H_M,
    H_M,
    W_M,
    W_M,
)
N_COLS = sum(SPLITS)

kernel_name = 'hymba_dsa_mlstm_deepnorm'


def layer_norm(x, g, b):
    xf = x.astype(jnp.float32)
    mu = jnp.mean(xf, axis=-1, keepdims=True)
    var = jnp.mean(jnp.square(xf - mu), axis=-1, keepdims=True)
    return ((xf - mu) * lax.rsqrt(var + LN_EPS) * g + b).astype(x.dtype)


def rms_norm(x, g):
    xf = x.astype(jnp.float32)
    return (xf * lax.rsqrt(jnp.mean(xf * xf, axis=-1, keepdims=True) + LN_EPS) * g).astype(x.dtype)


def causal_dwconv(x, w, b):
    T = x.shape[1]
    xp = jnp.pad(x, ((0, 0), (CONV_W - 1, 0), (0, 0)))
    return sum(w[j] * xp[:, j:j + T] for j in range(CONV_W)) + b


def t5_bucket(rel):
    max_exact = N_BUCKETS // 2
    n = jnp.maximum(rel, 0)
    nf = jnp.maximum(n, 1).astype(jnp.float32)
    large = max_exact + (jnp.log(nf / max_exact) / math.log(MAX_DIST / max_exact)
                         * (N_BUCKETS - max_exact)).astype(jnp.int32)
    large = jnp.minimum(large, N_BUCKETS - 1)
    return jnp.where(n < max_exact, n, large)


def dsa_attention(q_lat, ckv, q_idx, k_idx, w_idx, rel_bias):
    B, T = ckv.shape[:2]
    n_sel = min(TOPK_MAX, T // 4)
    nb = T // Q_BLOCK
    key_pos = jnp.arange(T)

    def block(args):
        qb, qib, wb, start = args
        qpos = start + jnp.arange(Q_BLOCK)
        sc = jnp.einsum('bqhd,bsd->bqhs', qib, k_idx)
        score = jnp.einsum('bqh,bqhs->bqs', wb, jax.nn.relu(sc)).astype(jnp.float32)
        causal = key_pos[None, :] <= qpos[:, None]
        score = jnp.where(causal[None], score, -jnp.inf)
        _, sel = lax.top_k(score, n_sel)
        c_sel = jax.vmap(lambda c, i: c[i])(ckv, sel)
        logits = jnp.einsum('bqhc,bqkc->bhqk', qb, c_sel).astype(jnp.float32) * (DH_A ** -0.5)
        rel = qpos[None, :, None] - sel
        bias = rel_bias[t5_bucket(rel)]
        logits = logits + jnp.transpose(bias, (0, 3, 1, 2)).astype(jnp.float32)
        logits = jnp.where((rel >= 0)[:, None], logits, -jnp.inf)
        p = jax.nn.softmax(logits, axis=-1).astype(ckv.dtype)
        return jnp.einsum('bhqk,bqkc->bqhc', p, c_sel)

    def to_blocks(a):
        return jnp.moveaxis(a.reshape((B, nb, Q_BLOCK) + a.shape[2:]), 1, 0)

    out = lax.map(block, (to_blocks(q_lat), to_blocks(q_idx), to_blocks(w_idx),
                          jnp.arange(nb) * Q_BLOCK))
    return jnp.moveaxis(out, 0, 1).reshape(B, T, H_A, D_C)


def mlstm_chunkwise(q, k, v, i_pre, f_pre):
    B, T, H, DK = q.shape
    DV = v.shape[-1]
    nc = T // CHUNK
    f32 = jnp.float32

    def chunks(a):
        a = a.reshape((B, nc, CHUNK, H) + a.shape[3:])
        return jnp.moveaxis(jnp.moveaxis(a, 1, 0), 3, 2)

    qc = chunks(q.astype(f32))
    kc = chunks(k.astype(f32) * (DK ** -0.5))
    vc = chunks(v.astype(f32))
    ic = chunks(i_pre.astype(f32))
    lfc = chunks(jax.nn.log_sigmoid(f_pre.astype(f32)))
    tri = jnp.tril(jnp.ones((CHUNK, CHUNK), dtype=bool))

    def step(carry, xs):
        C, n, m = carry
        qq, kk, vv, ii, lf = xs
        b = jnp.cumsum(lf, axis=-1)
        logD = jnp.where(tri, b[..., :, None] - b[..., None, :] + ii[..., None, :], -jnp.inf)
        g = b + m[..., None]
        m_t = jnp.maximum(jnp.max(logD, axis=-1), g)
        S = jnp.einsum('bhld,bhsd->bhls', qq, kk) * jnp.exp(logD - m_t[..., None])
        inter = jnp.exp(g - m_t)
        num = jnp.einsum('bhls,bhsv->bhlv', S, vv) + inter[..., None] * jnp.einsum('bhld,bhvd->bhlv', qq, C)
        den = jnp.sum(S, axis=-1) + inter * jnp.einsum('bhld,bhd->bhl', qq, n)
        h = num / jnp.maximum(jnp.abs(den), jnp.exp(-m_t))[..., None]
        bL = b[..., -1]
        a = bL[..., None] - b + ii
        m_new = jnp.maximum(bL + m, jnp.max(a, axis=-1))
        decay = jnp.exp(bL + m - m_new)
        wgt = jnp.exp(a - m_new[..., None])
        C_new = decay[..., None, None] * C + jnp.einsum('bhl,bhlv,bhld->bhvd', wgt, vv, kk)
        n_new = decay[..., None] * n + jnp.einsum('bhl,bhld->bhd', wgt, kk)
        return (C_new, n_new, m_new), h

    init = (jnp.zeros((B, H, DV, DK), f32), jnp.zeros((B, H, DK), f32), jnp.zeros((B, H), f32))
    _, hs = lax.scan(step, init, (qc, kc, vc, ic, lfc))
    return jnp.transpose(hs, (1, 0, 3, 2, 4)).reshape(B, T, H, DV).astype(q.dtype)


def setup_inputs(seed: int = 0) -> dict:
    key = jax.random.key(seed)
    ks = jax.random.split(key, 20)
    f32 = jnp.float32
    nrm = lambda k, s, sc: jax.random.normal(k, s, f32) * sc
    return {
        'x': nrm(ks[0], (BATCH, SEQ, D_MODEL), 1.0),
        'w_in': nrm(ks[1], (DEPTH, D_MODEL, N_COLS), D_MODEL ** -0.5),
        'b_igate': nrm(ks[2], (DEPTH, H_M), 0.1),
        'b_fgate': jnp.linspace(3.0, 6.0, H_M, dtype=f32)[None] + nrm(ks[3], (DEPTH, H_M), 0.1),
        'kv_norm_g': 1.0 + nrm(ks[4], (DEPTH, D_C), 0.02),
        'w_uk': nrm(ks[5], (DEPTH, H_A, DH_A, D_C), D_C ** -0.5),
        'w_uv': nrm(ks[6], (DEPTH, H_A, D_C, DH_A), D_C ** -0.5),
        'idx_k_ln_g': 1.0 + nrm(ks[7], (DEPTH, D_IDX), 0.02),
        'idx_k_ln_b': nrm(ks[8], (DEPTH, D_IDX), 0.02),
        'rel_bias': nrm(ks[9], (N_BUCKETS, H_A), 0.5),
        'conv_w': nrm(ks[10], (DEPTH, CONV_W, 2 * H_M * DK_M), CONV_W ** -0.5),
        'conv_b': nrm(ks[11], (DEPTH, 2 * H_M * DK_M), 0.01),
        'mh_norm_g': 1.0 + nrm(ks[12], (DEPTH, W_M), 0.02),
        'w_out': nrm(ks[13], (DEPTH, W_A + W_M, D_MODEL), BETA * (W_A + W_M) ** -0.5),
        'ln_g': 1.0 + nrm(ks[14], (DEPTH, D_MODEL), 0.02),
        'ln_b': nrm(ks[15], (DEPTH, D_MODEL), 0.02),
    }


def reference(x, w_in, b_igate, b_fgate, kv_norm_g, w_uk, w_uv, idx_k_ln_g, idx_k_ln_b,
              rel_bias, conv_w, conv_b, mh_norm_g, w_out, ln_g, ln_b):
    B, T, _ = x.shape
    offs = [int(o) for o in np.cumsum(SPLITS)[:-1]]
    for l in range(DEPTH):
        proj = x @ w_in[l]
        (q_a, c_kv, z_a, q_i, k_i, w_i, q_m, k_m, v_m,
         i_m, f_m, o_m, z_m) = jnp.split(proj, offs, axis=-1)

        c_kv = rms_norm(c_kv, kv_norm_g[l])
        q_lat = jnp.einsum('bthd,hdc->bthc', q_a.reshape(B, T, H_A, DH_A), w_uk[l])
        q_i = q_i.reshape(B, T, H_IDX, D_IDX) * (D_IDX ** -0.5)
        k_i = layer_norm(k_i, idx_k_ln_g[l], idx_k_ln_b[l])
        w_i = w_i * (H_IDX ** -0.5)
        o_lat = dsa_attention(q_lat, c_kv, q_i, k_i, w_i, rel_bias)
        y_a = jnp.einsum('bthc,hcd->bthd', o_lat, w_uv[l]).reshape(B, T, W_A) * jax.nn.silu(z_a)

        qk = jax.nn.silu(causal_dwconv(jnp.concatenate([q_m, k_m], axis=-1), conv_w[l], conv_b[l]))
        q_m, k_m = jnp.split(qk, 2, axis=-1)
        h_m = mlstm_chunkwise(q_m.reshape(B, T, H_M, DK_M), k_m.reshape(B, T, H_M, DK_M),
                              v_m.reshape(B, T, H_M, DV_M), i_m + b_igate[l], f_m + b_fgate[l])
        hf = h_m.astype(jnp.float32)
        mu = jnp.mean(hf, axis=-1, keepdims=True)
        var = jnp.mean(jnp.square(hf - mu), axis=-1, keepdims=True)
        h_m = ((hf - mu) * lax.rsqrt(var + LN_EPS)).reshape(B, T, W_M) * mh_norm_g[l]
        y_m = h_m.astype(x.dtype) * jax.nn.sigmoid(o_m) * jax.nn.silu(z_m)

        y = jnp.concatenate([y_a, y_m], axis=-1) @ w_out[l]
        x = layer_norm(ALPHA * x + y, ln_g[l], ln_b[l])
    return x
```

```python
import math
from contextlib import ExitStack

import numpy as np
import ml_dtypes
import concourse.bass as bass
import concourse.mybir as mybir
from concourse.bass_utils import run_bass_kernel_spmd

F32 = mybir.dt.float32
BF16 = mybir.dt.bfloat16
ALU = mybir.AluOpType
AF = mybir.ActivationFunctionType
AX = mybir.AxisListType

T = 2048
D = 2048
NCOLS = 7512
NIT = 15
TOPK = 256
LN_EPS = 1e-5
ALPHA = 2.0 ** 0.25
B0 = 16640
LIMIT = 212000

ENGS = ("pe", "act", "dve", "pool", "sp")


class Prog:
    def __init__(self, nc):
        self.nc = nc
        self.ops = {e: [] for e in ENGS}
        self.last_w = {}
        self.readers = {}
        self.slot_cnt = {}
        self.slots = []

    def op(self, eng, fn, reads=(), writes=(), slot=None, preads=()):
        isbank = lambda w: w.startswith("ps") and w[2:].isdigit()
        if eng != "pe":
            preads = list(preads) + [w for w in writes if isbank(w)]
            writes = [w for w in writes if not isbank(w)]
        deps = set()
        for r in reads:
            if r in self.last_w:
                deps.add(self.last_w[r])
        for r in preads:
            if r in self.last_w:
                deps.add(self.last_w[r])
            for rd in self.readers.get(r, ()):
                if rd[0] != eng:
                    deps.add(rd)
        for w in writes:
            if w in self.last_w:
                deps.add(self.last_w[w])
            for rd in self.readers.get(w, ()):
                deps.add(rd)
        is_dma = slot is not None
        if not is_dma and eng == "pe":
            deps = {d for d in deps if d[0] != "pe"}
        rec = dict(fn=fn, deps=deps, signal=False, dma=is_dma, slot=slot, sigval=None)
        if is_dma:
            if slot not in self.slot_cnt:
                self.slot_cnt[slot] = 0
                self.slots.append(slot)
            self.slot_cnt[slot] += 1
            rec["sigval"] = 16 * self.slot_cnt[slot]
        me = (eng, len(self.ops[eng]))
        self.ops[eng].append(rec)
        for w in writes:
            self.last_w[w] = me
            self.readers[w] = []
        for r in list(reads) + list(preads):
            if r not in writes:
                self.readers.setdefault(r, []).append(me)
        return me

    def barrier(self):
        lasts = set()
        for e in ENGS:
            for i in range(len(self.ops[e]) - 1, -1, -1):
                r = self.ops[e][i]
                if not r["dma"] and r["fn"] is not None:
                    lasts.add((e, i)); break
        dmas = set()
        for e in ENGS:
            seen = set()
            for i in range(len(self.ops[e]) - 1, -1, -1):
                r = self.ops[e][i]
                if r["dma"] and r["slot"] not in seen:
                    seen.add(r["slot"]); dmas.add((e, i))
        for e in ENGS:
            deps = {d for d in (lasts | dmas) if d[0] != e or self.ops[d[0]][d[1]]["dma"]}
            self.ops[e].append(dict(fn=None, deps=deps, signal=False, dma=False, slot=None, sigval=None))

    def emit(self, stack):
        nc = self.nc
        for e in ENGS:
            for rec in self.ops[e]:
                for (de, di) in rec["deps"]:
                    self.ops[de][di]["signal"] = True
        sem_e = {e: stack.enter_context(nc.semaphore("s_" + e)) for e in ENGS}
        sem_slot = {s: stack.enter_context(nc.semaphore("d_" + s)) for s in self.slots}
        for e in ENGS:
            c = 0
            for rec in self.ops[e]:
                if rec["dma"] or rec["fn"] is None:
                    continue
                if rec["signal"]:
                    c += 1
                    rec["sigval"] = c
        block = stack.enter_context(nc.Block())
        engobj = {"pe": block.tensor, "act": block.scalar, "dve": block.vector,
                  "pool": block.gpsimd, "sp": block.sync}
        for e in ENGS:
            ops = self.ops[e]

            def body(eng, ops=ops, e=e):
                waited = {}
                for rec in ops:
                    need = {}
                    for (de, di) in rec["deps"]:
                        d = self.ops[de][di]
                        if d["dma"]:
                            key = ("slot", d["slot"]); sem = sem_slot[d["slot"]]
                        else:
                            if d["sigval"] is None:
                                continue
                            key = ("eng", de); sem = sem_e[de]
                        v = d["sigval"]
                        if v > need.get(key, (None, 0))[1]:
                            need[key] = (sem, v)
                    for key, (sem, v) in need.items():
                        if waited.get(key, 0) >= v:
                            continue
                        eng.wait_ge(sem, v)
                        waited[key] = v
                    if rec["fn"] is None:
                        continue
                    ins = rec["fn"](eng)
                    if rec["dma"]:
                        ins.then_inc(sem_slot[rec["slot"]], 16)
                    elif rec["signal"]:
                        ins.then_inc(sem_e[e], 1)

            engobj[e](body)


class Mem:
    def __init__(self, nc):
        self.nc = nc
        self.allocs = []

    def alloc(self, name, shape, dtype, off, life):
        esz = 4 if dtype == F32 else 2
        size = esz * int(np.prod(shape[1:]))
        assert off % 32 == 0, (name, off)
        assert off >= 0 and off + size <= LIMIT, (name, off, size)
        for (n2, o2, s2, l2) in self.allocs:
            if l2[0] <= life[1] and life[0] <= l2[1] and o2 < off + size and off < o2 + s2:
                raise AssertionError(f"SBUF overlap {name} vs {n2}")
        self.allocs.append((name, off, size, life))
        return self.nc.alloc_sbuf_tensor_at(name, list(shape), dtype, offset=B0 + off)


class Bump:
    def __init__(self, mem, start, end, life):
        self.mem, self.cur, self.end, self.life = mem, start, end, life

    def __call__(self, name, shape, dtype):
        esz = 4 if dtype == F32 else 2
        size = esz * int(np.prod(shape[1:]))
        size = (size + 63) // 64 * 64
        off = self.cur
        assert off + size <= self.end, (name, off, size, self.end)
        self.cur += size
        return self.mem.alloc(name, shape, dtype, off, self.life)


KB = 1024
SEGS = [
    ("ckv", 1024, 256, "FM"), ("kiw", 3328, 80, "TM"),
    ("qi0", 2304, 512, "FM"), ("qi1", 2816, 512, "FM"),
    ("qa0", 0, 512, "FM"), ("qa1", 512, 512, "FM"),
    ("za0", 1280, 512, "TM"), ("za1", 1792, 512, "TM"),
    ("qm", 3408, 512, "FM"), ("km", 3920, 512, "FM"),
    ("if", 5456, 8, "TM"),
    ("vm0", 4432, 512, "TM"), ("vm1", 4944, 512, "TM"),
    ("om0", 5464, 512, "TM"), ("om1", 5976, 512, "TM"),
    ("zm0", 6488, 512, "TM"), ("zm1", 7000, 512, "TM"),
]
PP_KVG, PP_CONVW, PP_CONVB, PP_B31 = 0, 2, 34, 42
PP_N = 64
BC_IDXG, BC_IDXB, BC_BI, BC_BF, BC_MHG, BC_LNG, BC_LNB = 0, 64, 128, 132, 136, 1160, 3208
BC_N = 5256


def build_program(upto=9, dbg=()):
    nc = bass.Bass("TRN2", target_bir_lowering=False)
    dt_in = lambda name, shape, dt=F32: nc.dram_tensor(name, list(shape), dt, kind="ExternalInput").ap()
    x_d = dt_in("x", [T, D])
    win_d = dt_in("w_in", [D, NCOLS])
    wout_d = dt_in("w_out", [D, D])
    wuk_d = dt_in("w_ukT", [256, 1024])
    wuv_d = dt_in("w_uv2", [256, 1024])
    bias_d = dt_in("biasT", [8, 2, 128, 128])
    pp_d = dt_in("pp", [128, PP_N])
    bc_d = dt_in("bc", [128, BC_N])
    cst_d = dt_in("cst", [128, 5, 128])
    out_d = nc.dram_tensor("out", [T, D], F32, kind="ExternalOutput").ap()
    scr = lambda name, shape, dt=BF16: nc.dram_tensor(name, list(shape), dt, kind="Internal").ap()
    qaT_d = scr("qaT_d", [8, 128, T])
    qiT_d = scr("qiT_d", [8, 128, T])
    qkT_d = scr("qkT_d", [8, 128, T])
    ga_d = scr("ga_d", [T, 1024])
    vm_d = scr("vm_d", [T, 1024])
    gom_d = scr("gom_d", [T, 1024])
    gzm_d = scr("gzm_d", [T, 1024])
    dbg_out = {}

    P = Prog(nc)
    mem = Mem(nc)
    PS = nc.alloc_psum_tensor("PS", [128, 4096], F32)
    bank = lambda b, n=512, o=0: PS[:, b * 512 + o: b * 512 + o + n]
    bname = lambda b: f"ps{b}"

    def DMA(q, out, in_, reads, writes, slot):
        P.op(q, lambda e: e.dma_start(out=out, in_=in_), reads=reads, writes=writes, slot=slot)

    def MM(out, lhsT, rhs, start, stop, reads, b):
        P.op("pe", lambda e: e.matmul(out, lhsT, rhs, start=start, stop=stop), reads=reads, writes=[bname(b)])

    def TR(out, in_, ident, reads, b):
        P.op("pe", lambda e: e.transpose(out, in_, ident), reads=reads, writes=[bname(b)])

    def ACT(out, in_, func, reads, writes, scale=1.0, bias=0.0, eng="act"):
        P.op(eng, lambda e: e.activation(out=out, in_=in_, func=func, bias=bias, scale=scale),
             reads=reads, writes=writes)

    def TS(eng, out, in0, s1, s2, op0, op1, reads, writes, accum=None):
        if op1 is None:
            P.op(eng, lambda e: e.tensor_scalar(out, in0, s1, None, op0), reads=reads, writes=writes)
        elif accum is None:
            P.op(eng, lambda e: e.tensor_scalar(out, in0, s1, s2, op0, op1), reads=reads, writes=writes)
        else:
            P.op(eng, lambda e: e.tensor_scalar(out, in0, s1, s2, op0, op1, accum_out=accum),
                 reads=reads, writes=writes)

    def STT(eng, out, in0, s, in1, op0, op1, reads, writes):
        P.op(eng, lambda e: e.scalar_tensor_tensor(out, in0, s, in1, op0, op1), reads=reads, writes=writes)

    def TT(eng, out, in0, in1, op, reads, writes):
        P.op(eng, lambda e: e.tensor_tensor(out, in0, in1, op), reads=reads, writes=writes)

    def CP(eng, out, in_, reads, writes):
        if eng == "act":
            ACT(out, in_, AF.Copy, reads, writes)
        else:
            P.op(eng, lambda e: e.tensor_copy(out, in_), reads=reads, writes=writes)

    def MS(eng, ap, val, writes):
        P.op(eng, lambda e: e.memset(ap, val), writes=writes)

    def dump(name, src_ap, shape, dtype, reads):
        d = nc.dram_tensor("dbg_" + name, list(shape), dtype, kind="ExternalOutput").ap()
        dbg_out[name] = d
        DMA("sp", d, src_ap, reads, ["dbg_" + name], "dbg_" + name)

    ALL = (0, 5)
    cst = mem.alloc("cst", [128, 5, 128], F32, 0, ALL)
    identb = mem.alloc("identb", [128, 128], BF16, 2560, ALL)
    c01T = mem.alloc("c01T", [128, 128], F32, 2816, ALL)
    pp = mem.alloc("pp", [128, PP_N], F32, 3328, ALL)
    ifg = mem.alloc("ifg", [128, 16, 8], F32, 3584, (0, 4))
    ident, ones, tri, cnegTM, cnegT = (cst[:, i, :] for i in range(5))
    epsT = mem.alloc("epsT", [128, 1], F32, 4096, ALL)
    c01Tb = mem.alloc("c01Tb", [128, 128], BF16, 4096 + 64, ALL)
    kw = mem.alloc("kw", [128, 16, 80], F32, 146 * KB, (0, 2))
    ckv_raw = mem.alloc("ckv_raw", [128, 2, T], F32, 130 * KB, (0, 1))
    kiT2 = mem.alloc("kiT2", [128, T], BF16, 151 * KB, (1, 2))
    wi = mem.alloc("wi", [128, 16, 16], F32, 155 * KB, (1, 2))
    wukb = mem.alloc("wukb", [128, 2, 1024], BF16, 156 * KB, (1, 3))
    wuvb = mem.alloc("wuvb", [128, 2, 1024], BF16, 160 * KB, (1, 3))
    cT = mem.alloc("cT", [128, 2, T], BF16, 164 * KB, (1, 3))
    MT_N = 17408
    maskT = mem.alloc("maskT", [128, MT_N], BF16, 172 * KB, (2, 3))
    yT = mem.alloc("yT", [128, 16, T], BF16, 5 * KB, (3, 5))
    mt_off = [sum(T - 128 * k for k in range(kb)) for kb in range(17)]

    DMA("sp", cst[:], cst_d, [], ["cst"], "cst")
    DMA("sp", pp[:], pp_d, [], ["pp"], "pp")
    CP("dve", identb[:], ident, ["cst"], ["identb"])
    TS("dve", c01T[:], cnegT, 0.0, None, ALU.is_ge, None, ["cst"], ["c01T"])
    MS("pool", epsT[:], LN_EPS, ["epsT"])
    CP("dve", c01Tb[:], c01T[:], ["c01T"], ["c01Tb"])

    lo = Bump(mem, 5 * KB, 130 * KB, (0, 0))
    hi = Bump(mem, 151 * KB, LIMIT, (0, 0))
    xT = lo("xT", [128, 16, T], BF16)
    wst = lo("wst", [128, 16, 512], F32)
    xin = [lo(f"xin{i}", [128, D], F32) for i in range(2)]
    ev = [lo(f"ev{i}", [128, 2048], BF16) for i in range(2)]
    wbf = [hi(f"wbf{i}", [128, 16, 512], BF16) for i in range(2)]
    cin = hi("cin", [128, T + 3], F32)
    cacc = hi("cacc", [128, T], F32)
    win_r = win_d.rearrange("(dc p) n -> p dc n", p=128)

    def load_w(i):
        _, c0, n, _ = SEGS[i]
        DMA("sp", wst[:, 0:8, 0:n], win_r[:, 0:8, c0:c0 + n], [], ["wstA"], "wstA")
        DMA("pool", wst[:, 8:16, 0:n], win_r[:, 8:16, c0:c0 + n], [], ["wstB"], "wstB")

    load_w(0)
    for tb in range(16):
        s = tb % 2
        DMA("sp", xin[s][:], x_d[tb * 128:(tb + 1) * 128, :], [], [f"xin{s}"], f"xin{s}")
        for g in range(4):
            b = (tb * 4 + g) % 8
            for j in range(4):
                dc = g * 4 + j
                TR(bank(b, 128, j * 128), xin[s][:, dc * 128:(dc + 1) * 128], ident, [f"xin{s}", "cst"], b)
            CP("act" if g % 2 else "dve", xT[:, g * 4:(g + 1) * 4, tb * 128:(tb + 1) * 128],
               bank(b).rearrange("p (a c) -> p a c", a=4), [], [bname(b), f"xT{tb // 4}"])
    MS("pool", cin[:, 0:3], 0.0, ["cin"])

    setc = [0]
    evc = [0]

    def next_set():
        s_ = setc[0] % 2
        setc[0] += 1
        return s_

    def next_ev():
        k = evc[0] % 2
        evc[0] += 1
        return k

    set_banks = lambda st: [bname(st * 4 + j) for j in range(4)]
    set_ap = lambda st: PS[:, st * 2048:(st + 1) * 2048]

    for i, (name, c0, n, kind) in enumerate(SEGS):
        ws = i % 2
        CP("dve", wbf[ws][:, :, 0:n], wst[:, :, 0:n], ["wstA", "wstB"], [f"wbf{ws}"])
        if i + 1 < len(SEGS):
            load_w(i + 1)
        if kind == "FM":
            for cc in range(n // 128):
                st = next_set()
                for dc in range(16):
                    for tq in range(4):
                        MM(bank(st * 4 + tq), wbf[ws][:, dc, cc * 128:(cc + 1) * 128],
                           xT[:, dc, tq * 512:(tq + 1) * 512], dc == 0, dc == 15,
                           [f"wbf{ws}", f"xT{tq}"], st * 4 + tq)
                sb = set_banks(st)
                if name == "ckv":
                    CP("act", ckv_raw[:, cc, :], set_ap(st), [], sb + ["ckv_raw"])
                elif name in ("qi0", "qi1", "qa0", "qa1"):
                    k = next_ev()
                    idx = (0 if name[2] == "0" else 4) + cc
                    sc_ = 0.125 if name[1] == "i" else 128.0 ** -0.5
                    dst = (qiT_d if name[1] == "i" else qaT_d)[idx]
                    ACT(ev[k][:], set_ap(st), AF.Copy, [], sb + [f"ev{k}"], scale=sc_)
                    DMA("sp", dst, ev[k][:], [f"ev{k}"], [f"{name[:2]}T_d{idx}"], f"ev{k}")
                else:
                    ch = (0 if name == "qm" else 4) + cc
                    cw = lambda j: pp[:, PP_CONVW + ch * 4 + j: PP_CONVW + ch * 4 + j + 1]
                    CP("act", cin[:, 3:T + 3], set_ap(st), [], sb + ["cin"])
                    TS("dve", cacc[:], cin[:, 0:T], cw(0), pp[:, PP_CONVB + ch:PP_CONVB + ch + 1],
                       ALU.mult, ALU.add, ["cin", "pp"], ["cacc"])
                    for j in range(1, 4):
                        STT("dve", cacc[:], cin[:, j:T + j], cw(j), cacc[:], ALU.mult, ALU.add,
                            ["cin", "pp", "cacc"], ["cacc"])
                    k = next_ev()
                    ACT(ev[k][:], cacc[:], AF.Silu, ["cacc"], [f"ev{k}"])
                    DMA("sp", qkT_d[ch], ev[k][:], [f"ev{k}"], [f"qkT_d{ch}"], f"ev{k}")
        else:
            for tb4 in range(4):
                st = next_set()
                for j in range(4):
                    tb = tb4 * 4 + j
                    for dc in range(16):
                        MM(bank(st * 4 + j, n), xT[:, dc, tb * 128:(tb + 1) * 128], wbf[ws][:, dc, 0:n],
                           dc == 0, dc == 15, [f"wbf{ws}", f"xT{tb4}"], st * 4 + j)
                sb = set_banks(st)
                src = set_ap(st).rearrange("p (j c) -> p j c", j=4)[:, :, 0:n]
                if name == "kiw":
                    CP("dve", kw[:, tb4 * 4:(tb4 + 1) * 4, :], src, [], sb + ["kw"])
                elif name == "if":
                    CP("dve", ifg[:, tb4 * 4:(tb4 + 1) * 4, :], src, [], sb + ["ifg"])
                else:
                    k = next_ev()
                    func = {"za": AF.Silu, "vm": AF.Copy, "om": AF.Sigmoid, "zm": AF.Silu}[name[:2]]
                    dd_ = {"za": ga_d, "vm": vm_d, "om": gom_d, "zm": gzm_d}[name[:2]]
                    half = int(name[2])
                    evv = ev[k][:].rearrange("p (j c) -> p j c", j=4)
                    ACT(evv, src, func, [], sb + [f"ev{k}"])
                    dst = dd_.rearrange("(tb p) n -> p tb n", p=128)[:, tb4 * 4:(tb4 + 1) * 4,
                                                                     half * 512:(half + 1) * 512]
                    DMA("sp", dst, evv, [f"ev{k}"], [f"{name[:2]}_d"], f"ev{k}")
    P.barrier()
    if "ckv_raw" in dbg:
        dump("ckv_raw", ckv_raw[:], [128, 2, T], F32, ["ckv_raw"])
    if "kw" in dbg:
        dump("kw", kw[:], [128, 16, 80], F32, ["kw"])
        dump("ifg", ifg[:], [128, 16, 8], F32, ["ifg"])
    if upto <= 0:
        return finish(nc, P, dbg_out, out_d)

    a = Bump(mem, 5 * KB, 130 * KB, (1, 1))
    sq = a("sq", [128, 2, T], F32)
    rs = a("rs", [128, T], F32)
    wstg = a("wstg", [128, 2, 1024], F32)
    wstg2 = a("wstg2", [128, 2, 1024], F32)
    bcA = a("bcA", [128, 128], F32)
    m1 = a("m1", [128, 16, 1], F32)
    v1 = a("v1", [128, 16, 1], F32)
    cen = a("cen", [128, 16, 64], F32)
    sq2 = a("sq2", [128, 16, 64], F32)
    kdup = a("kdup", [128, 16, 128], F32)
    DMA("sp", wstg[:], wuk_d.rearrange("(j p) n -> p j n", p=128), [], ["wstg"], "wstg")
    DMA("pool", wstg2[:], wuv_d.rearrange("(j p) n -> p j n", p=128), [], ["wstg2"], "wstg2")
    DMA("sp", bcA[:], bc_d[:, BC_IDXG:BC_IDXG + 128], [], ["bcA"], "bcA")
    ACT(sq[:], ckv_raw[:], AF.Square, ["ckv_raw"], ["sq"])
    for tq in range(4):
        for j in range(2):
            MM(bank(tq), ones, sq[:, j, tq * 512:(tq + 1) * 512], j == 0, j == 1, ["cst", "sq"], tq)
    ACT(rs[:], PS[:, 0:2048], AF.Ln, ["epsT"], set_banks(0) + ["rs"], scale=1.0 / 256, bias=epsT[:])
    ACT(rs[:], rs[:], AF.Exp, ["rs"], ["rs"], scale=-0.5)
    for j in range(2):
        STT("dve", cT[:, j, :], ckv_raw[:, j, :], pp[:, PP_KVG + j:PP_KVG + j + 1], rs[:], ALU.mult, ALU.mult,
            ["ckv_raw", "pp", "rs"], ["cT"])
    CP("dve", wukb[:], wstg[:], ["wstg"], ["wukb"])
    CP("pool", wuvb[:], wstg2[:], ["wstg2"], ["wuvb"])
    ki = kw[:, :, 0:64]
    P.op("dve", lambda e: e.tensor_reduce(m1[:], ki, AX.X, ALU.add), reads=["kw"], writes=["m1"])
    TS("dve", m1[:], m1[:], -1.0 / 64, None, ALU.mult, None, ["m1"], ["m1"])
    TT("dve", cen[:], ki, m1[:].to_broadcast([128, 16, 64]), ALU.add, ["kw", "m1"], ["cen"])
    TT("dve", sq2[:], cen[:], cen[:], ALU.mult, ["cen"], ["sq2"])
    P.op("dve", lambda e: e.tensor_reduce(v1[:], sq2[:], AX.X, ALU.add), reads=["sq2"], writes=["v1"])
    ACT(v1[:], v1[:], AF.Ln, ["v1", "epsT"], ["v1"], scale=1.0 / 64, bias=epsT[:])
    ACT(v1[:], v1[:], AF.Exp, ["v1"], ["v1"], scale=-0.5)
    TT("dve", cen[:], cen[:], v1[:].to_broadcast([128, 16, 64]), ALU.mult, ["cen", "v1"], ["cen"])
    gview = bcA[:, 0:64].rearrange("p (o c) -> p o c", o=1).to_broadcast([128, 16, 64])
    bview = bcA[:, 64:128].rearrange("p (o c) -> p o c", o=1).to_broadcast([128, 16, 64])
    TT("dve", cen[:], cen[:], gview, ALU.mult, ["cen", "bcA"], ["cen"])
    TT("dve", kdup[:, :, 0:64], cen[:], bview, ALU.add, ["cen", "bcA"], ["kdup"])
    CP("pool", kdup[:, :, 64:128], kdup[:, :, 0:64], ["kdup"], ["kdup"])
    for tb in range(16):
        b = 4 + tb // 4
        TR(bank(b, 128, (tb % 4) * 128), kdup[:, tb, :], ident, ["kdup", "cst"], b)
        if tb % 4 == 3:
            CP("act", kiT2[:, (tb // 4) * 512:(tb // 4 + 1) * 512], bank(b), [], [bname(b), "kiT2"])
    TS("pool", wi[:], kw[:, :, 64:80], 0.25, None, ALU.mult, None, ["kw"], ["wi"])
    P.barrier()
    if "cT" in dbg:
        dump("cT", cT[:], [128, 2, T], BF16, ["cT"])
        dump("kiT2", kiT2[:], [128, T], BF16, ["kiT2"])
        dump("wi", wi[:], [128, 16, 16], F32, ["wi"])
    if upto <= 1:
        return finish(nc, P, dbg_out, out_d)

    s_ = Bump(mem, 5 * KB, 146 * KB, (2, 2))
    qs = [s_(f"qs{i}", [128, 16, 128], BF16) for i in range(2)]
    dg = [s_(f"dg{i}", [128, 16, 128], BF16) for i in range(2)]
    NR = 6
    Rb = [s_(f"R{i}", [128, 512], BF16) for i in range(NR)]
    scores2 = [s_(f"scores{i}", [128, 7424], F32) for i in range(2)]
    junk = [s_(f"junk{i}", [128, T], BF16) for i in range(3)]
    jc = [0]
    cmpb = [s_(f"cmpb{i}", [128, T], BF16) for i in range(2)]
    junkA = [s_(f"junkA{i}", [128, T], BF16) for i in range(2)]
    cc = [0]
    mtm = [s_(f"mtm{i}", [128, T], BF16) for i in range(2)]
    lo_t = s_(f"lo_t{g}", [128, 4], F32)
    w0_t = s_("w0_t", [128, 4], F32)
    mid_t = s_("mid_t", [128, 4], F32)
    cnt_t = s_("cnt_t", [128, 4], F32)
    ge_t = s_("ge_t", [128, 4], F32)
    hi_t = s_("hi_t", [128, 4], F32)
    qiT_r = qiT_d.rearrange("hp (r d) t -> d (hp r) t", r=2)
    MS("pool", maskT[:], 0.0, ["maskT"])
    for i in range(2):
        MS("dve", qs[i][64:128, :, :], 0.0, [f"qs{i}"])
    rc = [0]
    scb = [0]
    accb = [0]
    trb = [0]
    LAG = 2

    QORD = [4 * g + j for g in (3, 2, 1, 0) for j in range(4)]
    QPOS = {qb: i for i, qb in enumerate(QORD)}

    def diag_build(qb):
        sl = QPOS[qb] % 2
        DMA("sp", qs[sl][0:64, :, :], qiT_r[:, :, qb * 128:(qb + 1) * 128], [], [f"qs{sl}"], f"qs{sl}")
        for h in range(16):
            ACT(dg[sl][:, h, :], identb[:], AF.Copy, ["identb", "wi"], [f"dg{sl}_{h}"], scale=wi[:, qb, h:h + 1])

    lo_g = [s_(f"lo_g{i}", [128, 4], F32) for i in range(4)]

    def goffs(g):
        offs = {}
        o_ = 0
        for qb in range(4 * g, 4 * g + 4):
            offs[qb] = o_
            o_ += 128 * (qb + 1)
        return offs

    def indexer(g):
        scores = scores2[g % 2]
        offs = {}
        o_ = 0
        for qb in range(4 * g, 4 * g + 4):
            offs[qb] = o_
            o_ += 128 * (qb + 1)
        for qb in range(4 * g, 4 * g + 4):
            sl = QPOS[qb] % 2
            L = 128 * (qb + 1)
            if QPOS[qb] + 1 < 16:
                diag_build(QORD[QPOS[qb] + 1])
            for kc in range((L + 511) // 512):
                n = min(512, L - 512 * kc)
                ab = 4 + accb[0] % 2
                accb[0] += 1
                pend = []
                for step in range(16 + LAG):
                    if step < 16:
                        h = step
                        sb_ = scb[0] % 4
                        scb[0] += 1
                        MM(bank(sb_, n), qs[sl][:, h, :], kiT2[:, kc * 512:kc * 512 + n],
                           True, True, [f"qs{sl}", "kiT2"], sb_)
                        ri = rc[0] % NR
                        rc[0] += 1
                        if g == 3 and h % 2 == 1:
                            TS("dve", Rb[ri][:, 0:n], bank(sb_, n), 0.0, None, ALU.max, None, [],
                               [bname(sb_), f"R{ri}"])
                        else:
                            ACT(Rb[ri][:, 0:n], bank(sb_, n), AF.Relu, [], [bname(sb_), f"R{ri}"])
                        pend.append((h, ri))
                    if step >= LAG:
                        h, ri = pend[step - LAG]
                        MM(bank(ab, n), dg[sl][:, h, :], Rb[ri][:, 0:n], h == 0, h == 15, [f"dg{sl}_{h}", f"R{ri}"], ab)
                base = offs[qb] + kc * 512
                last = (kc == (L + 511) // 512 - 1)
                CP("act", scores[:, base:base + n], bank(ab, n), [], [bname(ab), f"sc{(qb // 4) % 2}_{qb % 4}"])
                if last:
                    TT("pool", scores[:, base + n - 128:base + n], scores[:, base + n - 128:base + n], cnegTM, ALU.add,
                       ["cst", f"sc{(qb // 4) % 2}_{qb % 4}"], [f"sc{(qb // 4) % 2}_{qb % 4}"])
    def bisect(g):
        scores = scores2[g % 2]
        offs = goffs(g)
        lo_t = lo_g[g]
        qbs = list(range(4 * g, 4 * g + 4))
        scr_names = [f"sc{(qb // 4) % 2}_{qb % 4}" for qb in qbs]
        MS("dve", lo_t[:], -1e29, [f"lo_t{g}"])
        MS("dve", w0_t[:], 0.0, ["w0_t"])
        MS("dve", cnt_t[:], 0.0, ["cnt_t"])
        for j, qb in enumerate(qbs):
            if qb < 2:
                continue
            L = 128 * (qb + 1)
            sv = scores[:, offs[qb]:offs[qb] + L]
            P.op("dve", lambda e, sv=sv, j=j: e.tensor_reduce(hi_t[:, j:j + 1], sv, AX.X, ALU.max),
                 reads=[f"sc{(qb // 4) % 2}_{qb % 4}"], writes=["hi_t"])
            P.op("dve", lambda e, sv=sv, j=j: e.tensor_reduce(lo_t[:, j:j + 1], sv[:, 0:256], AX.X, ALU.min),
                 reads=[f"sc{(qb // 4) % 2}_{qb % 4}"], writes=[f"lo_t{g}"])
            TT("dve", w0_t[:, j:j + 1], hi_t[:, j:j + 1], lo_t[:, j:j + 1], ALU.subtract, ["hi_t", f"lo_t{g}"], ["w0_t"])
        if g > 0 or True:
            for k in range(NIT):
                c_ = 0.5 ** (k + 1)
                STT("dve", mid_t[:], w0_t[:], c_, lo_t[:], ALU.mult, ALU.add, ["w0_t", f"lo_t{g}"], ["mid_t"])
                for j, qb in sorted(enumerate(qbs), key=lambda t: 0 if t[0] in (1, 2) else 1):
                    if qb < 2:
                        continue
                    L = 128 * (qb + 1)
                    scn = f"sc{(qb // 4) % 2}_{qb % 4}"
                    sv = scores[:, offs[qb]:offs[qb] + L]
                    if j in (1, 2):
                        ci = cc[0] % 2
                        cc[0] += 1
                        TS("dve", cmpb[ci][:, 0:L], sv, mid_t[:, j:j + 1], None, ALU.is_ge, None,
                           [scn, "mid_t"], [f"cmpb{ci}"])
                        P.op("act", lambda e, ci=ci, L=L, j=j: e.activation(
                            out=junkA[ci][:, 0:L], in_=cmpb[ci][:, 0:L], func=AF.Copy, accum_out=cnt_t[:, j:j + 1]),
                            reads=[f"cmpb{ci}", "cnt_t"], writes=[f"junkA{ci}", f"cnt_t{j}"])
                    else:
                        ji = jc[0] % 3
                        jc[0] += 1
                        TS("dve", junk[ji][:, 0:L], sv, mid_t[:, j:j + 1], None,
                           ALU.is_ge, ALU.add, [scn, "mid_t", "cnt_t"], [f"junk{ji}", f"cnt_t{j}"],
                           accum=cnt_t[:, j:j + 1])
                TS("dve", ge_t[:], cnt_t[:], float(TOPK), None, ALU.is_ge, None,
                   ["cnt_t"] + [f"cnt_t{j}" for j in range(4)], ["ge_t"])
                TT("dve", ge_t[:], ge_t[:], w0_t[:], ALU.mult, ["ge_t", "w0_t"], ["ge_t"])
                STT("dve", lo_t[:], ge_t[:], c_, lo_t[:], ALU.mult, ALU.add, ["ge_t", f"lo_t{g}"], [f"lo_t{g}"])
    def masks(g):
        scores = scores2[g % 2]
        offs = goffs(g)
        lo_t = lo_g[g]
        qbs = list(range(4 * g, 4 * g + 4))
        for j, qb in enumerate(qbs):
            L = 128 * (qb + 1)
            ms = qb % 2
            TS("dve", mtm[ms][:, 0:L], scores[:, offs[qb]:offs[qb] + L], lo_t[:, j:j + 1], None,
               ALU.is_ge, None, [f"sc{(qb // 4) % 2}_{qb % 4}", f"lo_t{g}"], [f"mtm{ms}"])
            for kb in range(qb + 1):
                tb_ = 6 + trb[0] % 2
                trb[0] += 1
                pt = bank(tb_).bitcast(BF16)[:, 0:128]
                TR(pt, mtm[ms][:, kb * 128:(kb + 1) * 128], identb[:], [f"mtm{ms}", "identb"], tb_)
                dst = maskT[:, mt_off[kb] + (qb - kb) * 128: mt_off[kb] + (qb - kb + 1) * 128]
                CP("act" if trb[0] % 2 else "dve", dst, pt, ["maskT"], [bname(tb_), f"maskT_{kb}_{qb}"])
        if "thr" in dbg:
            dump(f"thr{g}", lo_t[:], [128, 4], F32, [f"lo_t{g}"])
    diag_build(QORD[0])
    indexer(3)
    bisect(3)
    for g in (2, 1, 0):
        indexer(g)
        masks(g + 1)
        bisect(g)
    masks(0)
    P.barrier()
    if "maskT" in dbg:
        dump("maskT", maskT[:], [128, MT_N], BF16, ["maskT"])
    if upto <= 2:
        return finish(nc, P, dbg_out, out_d)

    t_ = Bump(mem, 69 * KB, 156 * KB, (3, 3))
    Qh = [t_(f"Qh{i}", [128, T], BF16) for i in range(2)]
    KTh = [t_(f"KTh{i}", [128, T], BF16) for i in range(2)]
    Vh = [t_(f"Vh{i}", [128, 16, 129], BF16) for i in range(2)]
    gah = [t_(f"gah{i}", [128, 16, 128], BF16) for i in range(2)]
    NPT = 32
    PT = [t_(f"PT{i}", [128, 512], BF16) for i in range(NPT)]
    NE = 4
    Eb = [t_(f"E{i}", [128, 512], BF16) for i in range(NE)]
    bst = t_("bst", [128, 8, 2, 128], F32)
    bT = t_("bT", [128, 8, 2, 128], BF16)
    rden = [t_(f"rden{i}", [128, 1], F32) for i in range(2)]
    ya = [t_(f"ya{i}", [128, 128], BF16) for i in range(2)]
    DMA("sp", bst[:], bias_d.rearrange("h d p c -> p h d c"), [], ["bst"], "bst")
    for h in range(8):
        for dl in range(2):
            if dl == 0:
                STT("dve", bT[:, h, 0, :], bst[:, h, 0, :], pp[:, PP_B31 + h:PP_B31 + h + 1], cnegT,
                    ALU.subtract, ALU.add, ["bst", "pp", "cst"], ["bT"])
            else:
                TS("dve", bT[:, h, 1, :], bst[:, h, 1, :], pp[:, PP_B31 + h:PP_B31 + h + 1], None,
                   ALU.subtract, None, ["bst", "pp"], ["bT"])
    for i in range(2):
        MS("pool", Vh[i][:], 1.0, [f"Vh{i}"])
    ga_r = ga_d.rearrange("(tb p) n -> p tb n", p=128)
    ptc, ec, stc, pvc, yc = [0], [0], [0], [0], [0]
    pendq = []

    def attn_units(vres, Vt, vcols, weight_fn, post_fn, nst=4):
        def emit_pv(Tb, chunks):
            for j in range(4):
                qb = 4 * Tb + j
                pvb = 4 + pvc[0] % 2
                pvc[0] += 1
                for kb in range(qb + 1):
                    MM(bank(pvb, vcols), PT[chunks[kb]][:, j * 128:(j + 1) * 128], Vt[:, kb, :],
                       kb == 0, kb == qb, [f"PT{chunks[kb]}", vres], pvb)
                fin = post_fn(qb, pvb)
                if pendq:
                    pendq.pop(0)()
                pendq.append(fin)

        prev = None
        for Tb in range(4):
            nkb = 4 * Tb + 4
            chunks = {}
            for kb in range(nkb):
                c0 = max(Tb * 512, kb * 128)
                ncols = (Tb + 1) * 512 - c0
                lo_ = c0 - Tb * 512
                stb = stc[0] % nst
                stc[0] += 1
                pi = ptc[0] % NPT
                ptc[0] += 1
                chunks[kb] = pi
                weight_fn(kb, Tb, c0, ncols, lo_, stb, pi)
            if prev is not None:
                emit_pv(*prev)
            prev = (Tb, chunks)
        emit_pv(*prev)

    for h in range(8):
        hs = h % 2
        DMA("sp", Qh[hs][:], qaT_d[h], [], [f"Qh{hs}"], f"Qh{hs}")
        DMA("pool", gah[hs][:], ga_r[:, :, h * 128:(h + 1) * 128], [], [f"gah{hs}"], f"gah{hs}")
        for tq in range(4):
            for j in range(2):
                MM(bank(tq), wukb[:, j, h * 128:(h + 1) * 128], cT[:, j, tq * 512:(tq + 1) * 512],
                   j == 0, j == 1, ["wukb", "cT"], tq)
        CP("act", KTh[hs][:], PS[:, 0:2048], [], set_banks(0) + [f"KTh{hs}"])
        for tb4 in range(4):
            for jj in range(4):
                tb = tb4 * 4 + jj
                for j in range(2):
                    MM(bank(tb4, 128, jj * 128), cT[:, j, tb * 128:(tb + 1) * 128], wuvb[:, j, h * 128:(h + 1) * 128],
                       j == 0 and jj == 0, j == 1, ["cT", "wuvb"], tb4)
            CP("dve", Vh[hs][:, tb4 * 4:(tb4 + 1) * 4, 0:128], bank(tb4).rearrange("p (a c) -> p a c", a=4),
               [], [bname(tb4), f"Vh{hs}"])

        def wfn(kb, Tb, c0, ncols, lo_, stb, pi, h=h, hs=hs):
            dls = [dl for dl in (0, 1) if 4 * Tb <= kb + dl < 4 * Tb + 4]
            MM(bank(stb, ncols, lo_), KTh[hs][:, kb * 128:(kb + 1) * 128], Qh[hs][:, c0:c0 + ncols],
               True, len(dls) == 0, [f"KTh{hs}", f"Qh{hs}"], stb)
            for ii, dl in enumerate(dls):
                jj = kb + dl - 4 * Tb
                MM(bank(stb, 128, jj * 128), identb[:], bT[:, h, dl, :], False, ii == len(dls) - 1,
                   ["identb", "bT"], stb)
            ei = ec[0] % NE
            ec[0] += 1
            ACT(Eb[ei][:, lo_:512], bank(stb, ncols, lo_), AF.Exp, [], [bname(stb), f"E{ei}"])
            mo = mt_off[kb] + (c0 - 128 * kb)
            TT("dve", PT[pi][:, lo_:512], Eb[ei][:, lo_:512], maskT[:, mo:mo + ncols], ALU.mult,
               [f"E{ei}", "maskT"], [f"PT{pi}"])

        def pfn(qb, pvb, h=h, hs=hs):
            yi = yc[0] % 2
            yc[0] += 1
            P.op("dve", lambda e: e.reciprocal(rden[yi][:], bank(pvb, 1, 128)), writes=[bname(pvb), f"rden{yi}"])
            STT("dve", ya[yi][:], bank(pvb, 128), rden[yi][:], gah[hs][:, qb, :], ALU.mult, ALU.mult,
                [f"rden{yi}", f"gah{hs}"], [bname(pvb), f"ya{yi}"])
            def fin():
                tb_ = 6 + yi
                pt = bank(tb_).bitcast(BF16)[:, 0:128]
                TR(pt, ya[yi][:], identb[:], [f"ya{yi}", "identb"], tb_)
                CP("act", yT[:, h, qb * 128:(qb + 1) * 128], pt, [], [bname(tb_), "yT"])
            return fin

        attn_units(f"Vh{hs}", Vh[hs], 129, wfn, pfn)
    while pendq:
        pendq.pop(0)()
    P.barrier()
    if "yTa" in dbg:
        dump("yTa", yT[:, 0:8, :], [128, 8, T], BF16, ["yT"])
    if upto <= 3:
        return finish(nc, P, dbg_out, out_d)

    m_ = Bump(mem, 69 * KB, LIMIT, (4, 4))
    bcM = m_("bcM", [128, 1032], F32)
    fpre = m_("fpre", [128, 16, 4], F32)
    ipre = m_("ipre", [128, 16, 4], F32)
    lfn = m_("lfn", [128, 16, 4], F32)
    lfx = m_("lfx", [128, 16, 4], F32)
    u_t = m_("u_t", [128, 16, 4], F32)
    rt = [m_(f"rt{i}", [128, 128], F32) for i in range(4)]
    qmh = [m_(f"qmh{i}", [128, T], BF16) for i in range(2)]
    kmh = [m_(f"kmh{i}", [128, T], BF16) for i in range(2)]
    Vmh = [m_(f"Vmh{i}", [128, 16, 257], BF16) for i in range(2)]
    gomh = m_("gomh", [128, 16, 256], BF16)
    gzmh = m_("gzmh", [128, 16, 256], BF16)
    gall = [m_(f"gall{i}", [128, 16, 256], BF16) for i in range(2)]
    Fbc = [m_(f"Fbc{i}", [128, T], F32) for i in range(2)]
    NW = 4
    Wb = [m_(f"W{i}", [128, 512], F32) for i in range(NW)]
    PT = [m_(f"PTm{i}", [128, 512], BF16) for i in range(NPT)]
    NPB = 3
    nsb = [m_(f"nsb{i}", [128, 257], F32) for i in range(NPB)]
    hn = [m_(f"hn{i}", [128, 256], F32) for i in range(NPB)]
    ym = [m_(f"ym{i}", [128, 256], BF16) for i in range(NPB)]
    st6 = [m_(f"st6{i}", [128, 6], F32) for i in range(NPB)]
    mv = [m_(f"mv{i}", [128, 2], F32) for i in range(NPB)]
    ddt = [m_(f"dd{i}", [128, 1], F32) for i in range(NPB)]
    t1t = [m_(f"t1{i}", [128, 1], F32) for i in range(NPB)]
    DMA("sp", bcM[:], bc_d[:, BC_BI:BC_BI + 1032], [], ["bcM"], "bcM")
    for i in range(2):
        MS("pool", Vmh[i][:], 1.0, [f"Vmh{i}"])
    bi_v = bcM[:, 0:4].rearrange("p (o c) -> p o c", o=1).to_broadcast([128, 16, 4])
    bf_v = bcM[:, 4:8].rearrange("p (o c) -> p o c", o=1).to_broadcast([128, 16, 4])
    TT("dve", ipre[:], ifg[:, :, 0:4], bi_v, ALU.add, ["ifg", "bcM"], ["ipre"])
    TT("dve", fpre[:], ifg[:, :, 4:8], bf_v, ALU.add, ["ifg", "bcM"], ["fpre"])
    ACT(lfn[:], fpre[:], AF.Exp, ["fpre"], ["lfn"], scale=-1.0)
    ACT(lfn[:], lfn[:], AF.Ln, ["lfn"], ["lfn"], bias=1.0)
    MS("dve", lfx[:, 0, :], 0.0, ["lfx"])
    for tb in range(1, 16):
        TT("dve", lfx[:, tb, :], lfx[:, tb - 1, :], lfn[:, tb - 1, :], ALU.add, ["lfx", "lfn"], ["lfx"])
    MM(bank(0, 64), tri, lfn[:].rearrange("p a b -> p (a b)"), True, False, ["cst", "lfn"], 0)
    MM(bank(0, 64), ones, lfx[:].rearrange("p a b -> p (a b)"), False, True, ["cst", "lfx"], 0)
    STT("dve", u_t[:].rearrange("p a b -> p (a b)"), bank(0, 64), math.log(128.0 ** -0.5),
        ipre[:].rearrange("p a b -> p (a b)"), ALU.add, ALU.add, ["ipre"], [bname(0), "u_t"])
    vm_r = vm_d.rearrange("(tb p) n -> p tb n", p=128)
    gom_r = gom_d.rearrange("(tb p) n -> p tb n", p=128)
    gzm_r = gzm_d.rearrange("(tb p) n -> p tb n", p=128)
    wc, rtc = [0], [0]
    for h in range(4):
        hs = h % 2
        DMA("sp", qmh[hs][:], qkT_d[h], [], [f"qmh{hs}"], f"qmh{hs}")
        DMA("pool", kmh[hs][:], qkT_d[4 + h], [], [f"kmh{hs}"], f"kmh{hs}")
        DMA("sp", Vmh[hs][:, :, 0:256], vm_r[:, :, h * 256:(h + 1) * 256], [], [f"Vmh{hs}"], f"Vmh{hs}")
        DMA("pool", gomh[:], gom_r[:, :, h * 256:(h + 1) * 256], [], ["gomh"], "gomh")
        DMA("sp", gzmh[:], gzm_r[:, :, h * 256:(h + 1) * 256], [], ["gzmh"], "gzmh")
        TT("pool", gall[hs][:], gomh[:], gzmh[:], ALU.mult, ["gomh", "gzmh"], [f"gall{hs}"])
        mg = bcM[:, 8 + h * 256: 8 + (h + 1) * 256].rearrange("p (o c) -> p o c", o=1).to_broadcast([128, 16, 256])
        TT("pool", gall[hs][:], gall[hs][:], mg, ALU.mult, [f"gall{hs}", "bcM"], [f"gall{hs}"])
        for tb in range(16):
            ri = rtc[0] % 4
            rtc[0] += 1
            TS("dve", rt[ri][:], tri, lfn[:, tb, h:h + 1], lfx[:, tb, h:h + 1], ALU.mult, ALU.add,
               ["cst", "lfn", "lfx"], [f"rt{ri}"])
            b = tb // 4
            MM(bank(b, 128, (tb % 4) * 128), ones, rt[ri][:], tb % 4 == 0, True, ["cst", f"rt{ri}"], b)
            if tb % 4 == 3:
                ACT(Fbc[hs][:, b * 512:(b + 1) * 512], bank(b), AF.Copy, [], [bname(b), f"Fbc{hs}"], scale=-1.0)

        def wfn(kb, Tb, c0, ncols, lo_, stb, pi, h=h, hs=hs):
            wi_ = wc[0] % NW
            wc[0] += 1
            ACT(Wb[wi_][:, lo_:512], Fbc[hs][:, c0:c0 + ncols], AF.Exp, [f"Fbc{hs}", "u_t"], [f"W{wi_}"],
                bias=u_t[:, kb, h:h + 1])
            MM(bank(stb, ncols, lo_), kmh[hs][:, kb * 128:(kb + 1) * 128], qmh[hs][:, c0:c0 + ncols],
               True, True, [f"kmh{hs}", f"qmh{hs}"], stb)
            TT("dve", PT[pi][:, lo_:512], bank(stb, ncols, lo_), Wb[wi_][:, lo_:512], ALU.mult,
               [f"W{wi_}"], [bname(stb), f"PT{pi}"])
            if kb >= 4 * Tb:
                TT("dve", PT[pi][:, lo_:lo_ + 128], PT[pi][:, lo_:lo_ + 128], c01Tb[:], ALU.mult,
                   [f"PT{pi}", "c01Tb"], [f"PT{pi}"])

        def pfn(qb, pvb, h=h, hs=hs):
            yi = yc[0] % NPB
            yc[0] += 1
            CP("act", nsb[yi][:], bank(pvb, 257), [], [bname(pvb), f"nsb{yi}"])
            Nap = nsb[yi][:, 0:256]
            nr = [f"nsb{yi}"]
            P.op("dve", lambda e: e.bn_stats(st6[yi][:], Nap), reads=nr, writes=[f"st6{yi}"])
            P.op("dve", lambda e: e.bn_aggr(mv[yi][:], st6[yi][:]), reads=[f"st6{yi}"], writes=[f"mv{yi}"])
            TT("dve", t1t[yi][:], nsb[yi][:, 256:257], nsb[yi][:, 256:257], ALU.mult, nr, [f"t1{yi}"])
            TS("dve", t1t[yi][:], t1t[yi][:], 1.0, LN_EPS, ALU.max, ALU.mult, [f"t1{yi}"], [f"t1{yi}"])
            TT("dve", t1t[yi][:], t1t[yi][:], mv[yi][:, 1:2], ALU.add, [f"t1{yi}", f"mv{yi}"], [f"t1{yi}"])
            ACT(t1t[yi][:], t1t[yi][:], AF.Ln, [f"t1{yi}"], [f"t1{yi}"])
            ACT(t1t[yi][:], t1t[yi][:], AF.Exp, [f"t1{yi}"], [f"t1{yi}"], scale=-0.5)
            TS("dve", hn[yi][:], Nap, mv[yi][:, 0:1], t1t[yi][:], ALU.subtract, ALU.mult,
               nr + [f"mv{yi}", f"t1{yi}"], [f"hn{yi}"])
            TT("dve", ym[yi][:], hn[yi][:], gall[hs][:, qb, :], ALU.mult, [f"hn{yi}", f"gall{hs}"], [f"ym{yi}"])
            def fin():
                tb_ = 6 + yi % 2
                pt = bank(tb_).bitcast(BF16)[:, 0:256]
                for i2 in range(2):
                    TR(pt[:, i2 * 128:(i2 + 1) * 128], ym[yi][:, i2 * 128:(i2 + 1) * 128], identb[:],
                       [f"ym{yi}", "identb"], tb_)
                CP("act", yT[:, 8 + 2 * h:10 + 2 * h, qb * 128:(qb + 1) * 128],
                   pt.rearrange("p (a c) -> p a c", a=2), [], [bname(tb_), "yT"])
            return fin

        attn_units(f"Vmh{hs}", Vmh[hs], 257, wfn, pfn)
    while pendq:
        pendq.pop(0)()
    P.barrier()
    if "yTm" in dbg:
        dump("yTm", yT[:, 8:16, :], [128, 8, T], BF16, ["yT"])
    if upto <= 4:
        return finish(nc, P, dbg_out, out_d)

    o_ = Bump(mem, 69 * KB, LIMIT, (5, 5))
    wo_bf = o_("wo_bf", [128, 16, D], BF16)
    wstO = [o_(f"wstO{i}", [128, 16, 256], F32) for i in range(2)]
    xres = [o_(f"xres{i}", [128, D], F32) for i in range(3)]
    lngb = o_("lngb", [128, 2 * D], F32)
    stO = [o_(f"stO{i}", [128, 4, 6], F32) for i in range(3)]
    mvO = [o_(f"mvO{i}", [128, 2], F32) for i in range(3)]
    rsO = [o_(f"rsO{i}", [128, 1], F32) for i in range(3)]
    wout_r = wout_d.rearrange("(fc p) n -> p fc n", p=128)
    DMA("pool", lngb[:], bc_d[:, BC_LNG:BC_LNG + 2 * D], [], ["lngb"], "lngb")
    for pc in range(8):
        s = pc % 2
        DMA("sp", wstO[s][:, 0:8, :], wout_r[:, 0:8, pc * 256:(pc + 1) * 256], [], [f"wstO{s}a"], f"wstO{s}a")
        DMA("pool", wstO[s][:, 8:16, :], wout_r[:, 8:16, pc * 256:(pc + 1) * 256], [], [f"wstO{s}b"], f"wstO{s}b")
        CP(("dve", "act")[pc % 2], wo_bf[:, :, pc * 256:(pc + 1) * 256], wstO[s][:],
           [f"wstO{s}a", f"wstO{s}b"], ["wo_bf"])
    for tb in range(16):
        s = tb % 3
        DMA("sp", xres[s][:], x_d[tb * 128:(tb + 1) * 128, :], [], [f"xres{s}"], f"xres{s}")
        st = next_set()
        for n4 in range(4):
            for fc in range(16):
                MM(bank(st * 4 + n4), yT[:, fc, tb * 128:(tb + 1) * 128], wo_bf[:, fc, n4 * 512:(n4 + 1) * 512],
                   fc == 0, fc == 15, ["yT", "wo_bf"], st * 4 + n4)
        STT("dve", xres[s][:], xres[s][:], ALPHA, set_ap(st), ALU.mult, ALU.add, [f"xres{s}"],
            set_banks(st) + [f"xres{s}"])
        for c4 in range(4):
            P.op("dve", lambda e, c4=c4, s=s: e.bn_stats(stO[s][:, c4, :], xres[s][:, c4 * 512:(c4 + 1) * 512]),
                 reads=[f"xres{s}"], writes=[f"stO{s}"])
        P.op("dve", lambda e, s=s: e.bn_aggr(mvO[s][:], stO[s][:].rearrange("p a b -> p (a b)")),
             reads=[f"stO{s}"], writes=[f"mvO{s}"])
        ACT(rsO[s][:], mvO[s][:, 1:2], AF.Ln, [f"mvO{s}", "epsT"], [f"rsO{s}"], bias=epsT[:])
        ACT(rsO[s][:], rsO[s][:], AF.Exp, [f"rsO{s}"], [f"rsO{s}"], scale=-0.5)
        TS("dve", xres[s][:], xres[s][:], mvO[s][:, 0:1], rsO[s][:], ALU.subtract, ALU.mult,
           [f"xres{s}", f"mvO{s}", f"rsO{s}"], [f"xres{s}"])
        TT("dve", xres[s][:], xres[s][:], lngb[:, 0:D], ALU.mult, [f"xres{s}", "lngb"], [f"xres{s}"])
        TT("pool", xres[s][:], xres[s][:], lngb[:, D:2 * D], ALU.add, [f"xres{s}", "lngb"], [f"xres{s}"])
        DMA("sp", out_d[tb * 128:(tb + 1) * 128, :], xres[s][:], [f"xres{s}"], [f"out{tb}"], f"xres{s}")
    return finish(nc, P, dbg_out, out_d)


def finish(nc, P, dbg_out, out_d):
    P.barrier()
    with ExitStack() as st:
        P.emit(st)
    return nc, dbg_out


def _t5_bucket(n):
    n = np.maximum(n, 0)
    nf = np.maximum(n, 1).astype(np.float32)
    large = 16 + (np.log(nf / 16) / math.log(128 / 16) * 16).astype(np.int32)
    large = np.minimum(large, 31)
    return np.where(n < 16, n, large)


def _host_inputs(inp):
    f = lambda a: np.ascontiguousarray(np.asarray(a, dtype=np.float32))
    w_ukT = f(np.asarray(inp["w_uk"])[0].transpose(2, 0, 1).reshape(256, 1024))
    w_uv2 = f(np.asarray(inp["w_uv"])[0].transpose(1, 0, 2).reshape(256, 1024))
    rel_bias = np.asarray(inp["rel_bias"], dtype=np.float32)
    p = np.arange(128)[:, None]
    c = np.arange(128)[None, :]
    biasT = np.zeros((8, 2, 128, 128), np.float32)
    for dl in range(2):
        nrel = 128 * dl + c - p
        g = rel_bias[_t5_bucket(nrel)]
        g = np.where((nrel >= 0)[..., None], g, np.float32(0))
        biasT[:, dl] = g.transpose(2, 0, 1)
    pp = np.zeros((128, PP_N), np.float32)
    pp[:, PP_KVG:PP_KVG + 2] = np.asarray(inp["kv_norm_g"])[0].reshape(2, 128).T
    cw = np.asarray(inp["conv_w"])[0]
    pp[:, PP_CONVW:PP_CONVW + 32] = cw.reshape(4, 8, 128).transpose(2, 1, 0).reshape(128, 32)
    pp[:, PP_CONVB:PP_CONVB + 8] = np.asarray(inp["conv_b"])[0].reshape(8, 128).T
    pp[:, PP_B31:PP_B31 + 8] = np.broadcast_to(rel_bias[31][None, :], (128, 8))
    bc = np.zeros((128, BC_N), np.float32)
    rep = lambda v: np.broadcast_to(np.asarray(v, dtype=np.float32).reshape(1, -1), (128, np.asarray(v).size))
    bc[:, BC_IDXG:BC_IDXG + 64] = rep(np.asarray(inp["idx_k_ln_g"])[0])
    bc[:, BC_IDXB:BC_IDXB + 64] = rep(np.asarray(inp["idx_k_ln_b"])[0])
    bc[:, BC_BI:BC_BI + 4] = rep(np.asarray(inp["b_igate"])[0])
    bc[:, BC_BF:BC_BF + 4] = rep(np.asarray(inp["b_fgate"])[0])
    bc[:, BC_MHG:BC_MHG + 1024] = rep(np.asarray(inp["mh_norm_g"])[0])
    bc[:, BC_LNG:BC_LNG + 2048] = rep(np.asarray(inp["ln_g"])[0])
    bc[:, BC_LNB:BC_LNB + 2048] = rep(np.asarray(inp["ln_b"])[0])
    cst = np.zeros((128, 5, 128), np.float32)
    cst[:, 0] = np.eye(128)
    cst[:, 1] = 1.0
    cst[:, 2] = (p <= c)
    cst[:, 3] = np.where(c <= p, 0.0, -1e30)
    cst[:, 4] = np.where(c >= p, 0.0, -30000.0)
    shared = dict(w_in=f(np.asarray(inp["w_in"])[0]), w_out=f(np.asarray(inp["w_out"])[0]),
                  w_ukT=w_ukT, w_uv2=w_uv2, biasT=biasT, pp=pp, bc=bc, cst=cst)
    x = np.asarray(inp["x"], dtype=np.float32)
    return [dict(shared, x=np.ascontiguousarray(x[i])) for i in range(8)]


def kernel(**inputs):
    nc, _ = build_program()
    in_maps = _host_inputs(inputs)
    res = run_bass_kernel_spmd(nc, in_maps, core_ids=list(range(8)))
    return np.stack([np.asarray(r["out"], dtype=np.float32) for r in res.results], axis=0)
```

```python
import math
from contextlib import ExitStack

import numpy as np
import ml_dtypes
import concourse.bass as bass
import concourse.mybir as mybir
from concourse.bass_utils import run_bass_kernel_spmd

F32 = mybir.dt.float32
BF16 = mybir.dt.bfloat16
ALU = mybir.AluOpType
AF = mybir.ActivationFunctionType
AX = mybir.AxisListType

T = 2048
D = 2048
NCOLS = 7512
NIT = 15
TOPK = 256
LN_EPS = 1e-5
ALPHA = 2.0 ** 0.25
B0 = 16640
LIMIT = 212000

ENGS = ("pe", "act", "dve", "pool", "sp")


class Prog:
    def __init__(self, nc):
        self.nc = nc
        self.ops = {e: [] for e in ENGS}
        self.last_w = {}
        self.readers = {}
        self.slot_cnt = {}
        self.slots = []

    def op(self, eng, fn, reads=(), writes=(), slot=None, preads=()):
        isbank = lambda w: w.startswith("ps") and w[2:].isdigit()
        if eng != "pe":
            preads = list(preads) + [w for w in writes if isbank(w)]
            writes = [w for w in writes if not isbank(w)]
        deps = set()
        for r in reads:
            if r in self.last_w:
                deps.add(self.last_w[r])
        for r in preads:
            if r in self.last_w:
                deps.add(self.last_w[r])
            for rd in self.readers.get(r, ()):
                if rd[0] != eng:
                    deps.add(rd)
        for w in writes:
            if w in self.last_w:
                deps.add(self.last_w[w])
            for rd in self.readers.get(w, ()):
                deps.add(rd)
        is_dma = slot is not None
        if not is_dma and eng == "pe":
            deps = {d for d in deps if d[0] != "pe"}
        rec = dict(fn=fn, deps=deps, signal=False, dma=is_dma, slot=slot, sigval=None)
        if is_dma:
            if slot not in self.slot_cnt:
                self.slot_cnt[slot] = 0
                self.slots.append(slot)
            self.slot_cnt[slot] += 1
            rec["sigval"] = 16 * self.slot_cnt[slot]
        me = (eng, len(self.ops[eng]))
        self.ops[eng].append(rec)
        for w in writes:
            self.last_w[w] = me
            self.readers[w] = []
        for r in list(reads) + list(preads):
            if r not in writes:
                self.readers.setdefault(r, []).append(me)
        return me

    def barrier(self):
        lasts = set()
        for e in ENGS:
            for i in range(len(self.ops[e]) - 1, -1, -1):
                r = self.ops[e][i]
                if not r["dma"] and r["fn"] is not None:
                    lasts.add((e, i)); break
        dmas = set()
        for e in ENGS:
            seen = set()
            for i in range(len(self.ops[e]) - 1, -1, -1):
                r = self.ops[e][i]
                if r["dma"] and r["slot"] not in seen:
                    seen.add(r["slot"]); dmas.add((e, i))
        for e in ENGS:
            deps = {d for d in (lasts | dmas) if d[0] != e or self.ops[d[0]][d[1]]["dma"]}
            self.ops[e].append(dict(fn=None, deps=deps, signal=False, dma=False, slot=None, sigval=None))

    def emit(self, stack):
        nc = self.nc
        for e in ENGS:
            for rec in self.ops[e]:
                for (de, di) in rec["deps"]:
                    self.ops[de][di]["signal"] = True
        sem_e = {e: stack.enter_context(nc.semaphore("s_" + e)) for e in ENGS}
        sem_slot = {s: stack.enter_context(nc.semaphore("d_" + s)) for s in self.slots}
        for e in ENGS:
            c = 0
            for rec in self.ops[e]:
                if rec["dma"] or rec["fn"] is None:
                    continue
                if rec["signal"]:
                    c += 1
                    rec["sigval"] = c
        block = stack.enter_context(nc.Block())
        engobj = {"pe": block.tensor, "act": block.scalar, "dve": block.vector,
                  "pool": block.gpsimd, "sp": block.sync}
        for e in ENGS:
            ops = self.ops[e]

            def body(eng, ops=ops, e=e):
                waited = {}
                for rec in ops:
                    need = {}
                    for (de, di) in rec["deps"]:
                        d = self.ops[de][di]
                        if d["dma"]:
                            key = ("slot", d["slot"]); sem = sem_slot[d["slot"]]
                        else:
                            if d["sigval"] is None:
                                continue
                            key = ("eng", de); sem = sem_e[de]
                        v = d["sigval"]
                        if v > need.get(key, (None, 0))[1]:
                            need[key] = (sem, v)
                    for key, (sem, v) in need.items():
                        if waited.get(key, 0) >= v:
                            continue
                        eng.wait_ge(sem, v)
                        waited[key] = v
                    if rec["fn"] is None:
                        continue
                    ins = rec["fn"](eng)
                    if rec["dma"]:
                        ins.then_inc(sem_slot[rec["slot"]], 16)
                    elif rec["signal"]:
                        ins.then_inc(sem_e[e], 1)

            engobj[e](body)


class Mem:
    def __init__(self, nc):
        self.nc = nc
        self.allocs = []

    def alloc(self, name, shape, dtype, off, life):
        esz = 4 if dtype == F32 else 2
        size = esz * int(np.prod(shape[1:]))
        assert off % 32 == 0, (name, off)
        assert off >= 0 and off + size <= LIMIT, (name, off, size)
        for (n2, o2, s2, l2) in self.allocs:
            if l2[0] <= life[1] and life[0] <= l2[1] and o2 < off + size and off < o2 + s2:
                raise AssertionError(f"SBUF overlap {name} vs {n2}")
        self.allocs.append((name, off, size, life))
        return self.nc.alloc_sbuf_tensor_at(name, list(shape), dtype, offset=B0 + off)


class Bump:
    def __init__(self, mem, start, end, life):
        self.mem, self.cur, self.end, self.life = mem, start, end, life

    def __call__(self, name, shape, dtype):
        esz = 4 if dtype == F32 else 2
        size = esz * int(np.prod(shape[1:]))
        size = (size + 63) // 64 * 64
        off = self.cur
        assert off + size <= self.end, (name, off, size, self.end)
        self.cur += size
        return self.mem.alloc(name, shape, dtype, off, self.life)


KB = 1024
SEGS = [
    ("ckv", 1024, 256, "FM"), ("kiw", 3328, 80, "TM"),
    ("qi0", 2304, 512, "FM"), ("qi1", 2816, 512, "FM"),
    ("qa0", 0, 512, "FM"), ("qa1", 512, 512, "FM"),
    ("za0", 1280, 512, "TM"), ("za1", 1792, 512, "TM"),
    ("qm", 3408, 512, "FM"), ("km", 3920, 512, "FM"),
    ("if", 5456, 8, "TM"),
    ("vm0", 4432, 512, "TM"), ("vm1", 4944, 512, "TM"),
    ("om0", 5464, 512, "TM"), ("om1", 5976, 512, "TM"),
    ("zm0", 6488, 512, "TM"), ("zm1", 7000, 512, "TM"),
]
PP_KVG, PP_CONVW, PP_CONVB, PP_B31 = 0, 2, 34, 42
PP_N = 64
BC_IDXG, BC_IDXB, BC_BI, BC_BF, BC_MHG, BC_LNG, BC_LNB = 0, 64, 128, 132, 136, 1160, 3208
BC_N = 5256


def build_program(upto=9, dbg=()):
    nc = bass.Bass("TRN2", target_bir_lowering=False)
    dt_in = lambda name, shape, dt=F32: nc.dram_tensor(name, list(shape), dt, kind="ExternalInput").ap()
    x_d = dt_in("x", [T, D])
    win_d = dt_in("w_in", [D, NCOLS])
    wout_d = dt_in("w_out", [D, D])
    wuk_d = dt_in("w_ukT", [256, 1024])
    wuv_d = dt_in("w_uv2", [256, 1024])
    bias_d = dt_in("biasT", [8, 2, 128, 128])
    pp_d = dt_in("pp", [128, PP_N])
    bc_d = dt_in("bc", [128, BC_N])
    cst_d = dt_in("cst", [128, 5, 128])
    out_d = nc.dram_tensor("out", [T, D], F32, kind="ExternalOutput").ap()
    scr = lambda name, shape, dt=BF16: nc.dram_tensor(name, list(shape), dt, kind="Internal").ap()
    qaT_d = scr("qaT_d", [8, 128, T])
    qiT_d = scr("qiT_d", [8, 128, T])
    qkT_d = scr("qkT_d", [8, 128, T])
    ga_d = scr("ga_d", [T, 1024])
    vm_d = scr("vm_d", [T, 1024])
    gom_d = scr("gom_d", [T, 1024])
    gzm_d = scr("gzm_d", [T, 1024])
    dbg_out = {}

    P = Prog(nc)
    mem = Mem(nc)
    PS = nc.alloc_psum_tensor("PS", [128, 4096], F32)
    bank = lambda b, n=512, o=0: PS[:, b * 512 + o: b * 512 + o + n]
    bname = lambda b: f"ps{b}"

    def DMA(q, out, in_, reads, writes, slot):
        P.op(q, lambda e: e.dma_start(out=out, in_=in_), reads=reads, writes=writes, slot=slot)

    def MM(out, lhsT, rhs, start, stop, reads, b):
        P.op("pe", lambda e: e.matmul(out, lhsT, rhs, start=start, stop=stop), reads=reads, writes=[bname(b)])

    def TR(out, in_, ident, reads, b):
        P.op("pe", lambda e: e.transpose(out, in_, ident), reads=reads, writes=[bname(b)])

    def ACT(out, in_, func, reads, writes, scale=1.0, bias=0.0, eng="act"):
        P.op(eng, lambda e: e.activation(out=out, in_=in_, func=func, bias=bias, scale=scale),
             reads=reads, writes=writes)

    def TS(eng, out, in0, s1, s2, op0, op1, reads, writes, accum=None):
        if op1 is None:
            P.op(eng, lambda e: e.tensor_scalar(out, in0, s1, None, op0), reads=reads, writes=writes)
        elif accum is None:
            P.op(eng, lambda e: e.tensor_scalar(out, in0, s1, s2, op0, op1), reads=reads, writes=writes)
        else:
            P.op(eng, lambda e: e.tensor_scalar(out, in0, s1, s2, op0, op1, accum_out=accum),
                 reads=reads, writes=writes)

    def STT(eng, out, in0, s, in1, op0, op1, reads, writes):
        P.op(eng, lambda e: e.scalar_tensor_tensor(out, in0, s, in1, op0, op1), reads=reads, writes=writes)

    def TT(eng, out, in0, in1, op, reads, writes):
        P.op(eng, lambda e: e.tensor_tensor(out, in0, in1, op), reads=reads, writes=writes)

    def CP(eng, out, in_, reads, writes):
        if eng == "act":
            ACT(out, in_, AF.Copy, reads, writes)
        else:
            P.op(eng, lambda e: e.tensor_copy(out, in_), reads=reads, writes=writes)

    def MS(eng, ap, val, writes):
        P.op(eng, lambda e: e.memset(ap, val), writes=writes)

    def dump(name, src_ap, shape, dtype, reads):
        d = nc.dram_tensor("dbg_" + name, list(shape), dtype, kind="ExternalOutput").ap()
        dbg_out[name] = d
        DMA("sp", d, src_ap, reads, ["dbg_" + name], "dbg_" + name)

    ALL = (0, 5)
    cst = mem.alloc("cst", [128, 5, 128], F32, 0, ALL)
    identb = mem.alloc("identb", [128, 128], BF16, 2560, ALL)
    c01T = mem.alloc("c01T", [128, 128], F32, 2816, ALL)
    pp = mem.alloc("pp", [128, PP_N], F32, 3328, ALL)
    ifg = mem.alloc("ifg", [128, 16, 8], F32, 3584, (0, 4))
    ident, ones, tri, cnegTM, cnegT = (cst[:, i, :] for i in range(5))
    epsT = mem.alloc("epsT", [128, 1], F32, 4096, ALL)
    c01Tb = mem.alloc("c01Tb", [128, 128], BF16, 4096 + 64, ALL)
    kw = mem.alloc("kw", [128, 16, 80], F32, 146 * KB, (0, 2))
    ckv_raw = mem.alloc("ckv_raw", [128, 2, T], F32, 130 * KB, (0, 1))
    kiT2 = mem.alloc("kiT2", [128, T], BF16, 151 * KB, (1, 2))
    wi = mem.alloc("wi", [128, 16, 16], F32, 155 * KB, (1, 2))
    wukb = mem.alloc("wukb", [128, 2, 1024], BF16, 156 * KB, (1, 3))
    wuvb = mem.alloc("wuvb", [128, 2, 1024], BF16, 160 * KB, (1, 3))
    cT = mem.alloc("cT", [128, 2, T], BF16, 164 * KB, (1, 3))
    MT_N = 17408
    maskT = mem.alloc("maskT", [128, MT_N], BF16, 172 * KB, (2, 3))
    yT = mem.alloc("yT", [128, 16, T], BF16, 5 * KB, (3, 5))
    mt_off = [sum(T - 128 * k for k in range(kb)) for kb in range(17)]

    DMA("sp", cst[:], cst_d, [], ["cst"], "cst")
    DMA("sp", pp[:], pp_d, [], ["pp"], "pp")
    CP("dve", identb[:], ident, ["cst"], ["identb"])
    TS("dve", c01T[:], cnegT, 0.0, None, ALU.is_ge, None, ["cst"], ["c01T"])
    MS("pool", epsT[:], LN_EPS, ["epsT"])
    CP("dve", c01Tb[:], c01T[:], ["c01T"], ["c01Tb"])

    lo = Bump(mem, 5 * KB, 130 * KB, (0, 0))
    hi = Bump(mem, 151 * KB, LIMIT, (0, 0))
    xT = lo("xT", [128, 16, T], BF16)
    wst = lo("wst", [128, 16, 512], F32)
    xin = [lo(f"xin{i}", [128, D], F32) for i in range(2)]
    ev = [lo(f"ev{i}", [128, 2048], BF16) for i in range(2)]
    wbf = [hi(f"wbf{i}", [128, 16, 512], BF16) for i in range(2)]
    cin = hi("cin", [128, T + 3], F32)
    cacc = hi("cacc", [128, T], F32)
    win_r = win_d.rearrange("(dc p) n -> p dc n", p=128)

    def load_w(i):
        _, c0, n, _ = SEGS[i]
        DMA("sp", wst[:, 0:8, 0:n], win_r[:, 0:8, c0:c0 + n], [], ["wstA"], "wstA")
        DMA("pool", wst[:, 8:16, 0:n], win_r[:, 8:16, c0:c0 + n], [], ["wstB"], "wstB")

    load_w(0)
    for tb in range(16):
        s = tb % 2
        DMA("sp", xin[s][:], x_d[tb * 128:(tb + 1) * 128, :], [], [f"xin{s}"], f"xin{s}")
        for g in range(4):
            b = (tb * 4 + g) % 8
            for j in range(4):
                dc = g * 4 + j
                TR(bank(b, 128, j * 128), xin[s][:, dc * 128:(dc + 1) * 128], ident, [f"xin{s}", "cst"], b)
            CP("act" if g % 2 else "dve", xT[:, g * 4:(g + 1) * 4, tb * 128:(tb + 1) * 128],
               bank(b).rearrange("p (a c) -> p a c", a=4), [], [bname(b), f"xT{tb // 4}"])
    MS("pool", cin[:, 0:3], 0.0, ["cin"])

    setc = [0]
    evc = [0]

    def next_set():
        s_ = setc[0] % 2
        setc[0] += 1
        return s_

    def next_ev():
        k = evc[0] % 2
        evc[0] += 1
        return k

    set_banks = lambda st: [bname(st * 4 + j) for j in range(4)]
    set_ap = lambda st: PS[:, st * 2048:(st + 1) * 2048]

    for i, (name, c0, n, kind) in enumerate(SEGS):
        ws = i % 2
        CP("dve", wbf[ws][:, :, 0:n], wst[:, :, 0:n], ["wstA", "wstB"], [f"wbf{ws}"])
        if i + 1 < len(SEGS):
            load_w(i + 1)
        if kind == "FM":
            for cc in range(n // 128):
                st = next_set()
                for dc in range(16):
                    for tq in range(4):
                        MM(bank(st * 4 + tq), wbf[ws][:, dc, cc * 128:(cc + 1) * 128],
                           xT[:, dc, tq * 512:(tq + 1) * 512], dc == 0, dc == 15,
                           [f"wbf{ws}", f"xT{tq}"], st * 4 + tq)
                sb = set_banks(st)
                if name == "ckv":
                    CP("act", ckv_raw[:, cc, :], set_ap(st), [], sb + ["ckv_raw"])
                elif name in ("qi0", "qi1", "qa0", "qa1"):
                    k = next_ev()
                    idx = (0 if name[2] == "0" else 4) + cc
                    sc_ = 0.125 if name[1] == "i" else 128.0 ** -0.5
                    dst = (qiT_d if name[1] == "i" else qaT_d)[idx]
                    ACT(ev[k][:], set_ap(st), AF.Copy, [], sb + [f"ev{k}"], scale=sc_)
                    DMA("sp", dst, ev[k][:], [f"ev{k}"], [f"{name[:2]}T_d{idx}"], f"ev{k}")
                else:
                    ch = (0 if name == "qm" else 4) + cc
                    cw = lambda j: pp[:, PP_CONVW + ch * 4 + j: PP_CONVW + ch * 4 + j + 1]
                    CP("act", cin[:, 3:T + 3], set_ap(st), [], sb + ["cin"])
                    TS("dve", cacc[:], cin[:, 0:T], cw(0), pp[:, PP_CONVB + ch:PP_CONVB + ch + 1],
                       ALU.mult, ALU.add, ["cin", "pp"], ["cacc"])
                    for j in range(1, 4):
                        STT("dve", cacc[:], cin[:, j:T + j], cw(j), cacc[:], ALU.mult, ALU.add,
                            ["cin", "pp", "cacc"], ["cacc"])
                    k = next_ev()
                    ACT(ev[k][:], cacc[:], AF.Silu, ["cacc"], [f"ev{k}"])
                    DMA("sp", qkT_d[ch], ev[k][:], [f"ev{k}"], [f"qkT_d{ch}"], f"ev{k}")
        else:
            for tb4 in range(4):
                st = next_set()
                for j in range(4):
                    tb = tb4 * 4 + j
                    for dc in range(16):
                        MM(bank(st * 4 + j, n), xT[:, dc, tb * 128:(tb + 1) * 128], wbf[ws][:, dc, 0:n],
                           dc == 0, dc == 15, [f"wbf{ws}", f"xT{tb4}"], st * 4 + j)
                sb = set_banks(st)
                src = set_ap(st).rearrange("p (j c) -> p j c", j=4)[:, :, 0:n]
                if name == "kiw":
                    CP("dve", kw[:, tb4 * 4:(tb4 + 1) * 4, :], src, [], sb + ["kw"])
                elif name == "if":
                    CP("dve", ifg[:, tb4 * 4:(tb4 + 1) * 4, :], src, [], sb + ["ifg"])
                else:
                    k = next_ev()
                    func = {"za": AF.Silu, "vm": AF.Copy, "om": AF.Sigmoid, "zm": AF.Silu}[name[:2]]
                    dd_ = {"za": ga_d, "vm": vm_d, "om": gom_d, "zm": gzm_d}[name[:2]]
                    half = int(name[2])
                    evv = ev[k][:].rearrange("p (j c) -> p j c", j=4)
                    ACT(evv, src, func, [], sb + [f"ev{k}"])
                    dst = dd_.rearrange("(tb p) n -> p tb n", p=128)[:, tb4 * 4:(tb4 + 1) * 4,
                                                                     half * 512:(half + 1) * 512]
                    DMA("sp", dst, evv, [f"ev{k}"], [f"{name[:2]}_d"], f"ev{k}")
    P.barrier()
    if "ckv_raw" in dbg:
        dump("ckv_raw", ckv_raw[:], [128, 2, T], F32, ["ckv_raw"])
    if "kw" in dbg:
        dump("kw", kw[:], [128, 16, 80], F32, ["kw"])
        dump("ifg", ifg[:], [128, 16, 8], F32, ["ifg"])
    if upto <= 0:
        return finish(nc, P, dbg_out, out_d)

    a = Bump(mem, 5 * KB, 130 * KB, (1, 1))
    sq = a("sq", [128, 2, T], F32)
    rs = a("rs", [128, T], F32)
    wstg = a("wstg", [128, 2, 1024], F32)
    wstg2 = a("wstg2", [128, 2, 1024], F32)
    bcA = a("bcA", [128, 128], F32)
    m1 = a("m1", [128, 16, 1], F32)
    v1 = a("v1", [128, 16, 1], F32)
    cen = a("cen", [128, 16, 64], F32)
    sq2 = a("sq2", [128, 16, 64], F32)
    kdup = a("kdup", [128, 16, 128], F32)
    DMA("sp", wstg[:], wuk_d.rearrange("(j p) n -> p j n", p=128), [], ["wstg"], "wstg")
    DMA("pool", wstg2[:], wuv_d.rearrange("(j p) n -> p j n", p=128), [], ["wstg2"], "wstg2")
    DMA("sp", bcA[:], bc_d[:, BC_IDXG:BC_IDXG + 128], [], ["bcA"], "bcA")
    ACT(sq[:], ckv_raw[:], AF.Square, ["ckv_raw"], ["sq"])
    for tq in range(4):
        for j in range(2):
            MM(bank(tq), ones, sq[:, j, tq * 512:(tq + 1) * 512], j == 0, j == 1, ["cst", "sq"], tq)
    ACT(rs[:], PS[:, 0:2048], AF.Ln, ["epsT"], set_banks(0) + ["rs"], scale=1.0 / 256, bias=epsT[:])
    ACT(rs[:], rs[:], AF.Exp, ["rs"], ["rs"], scale=-0.5)
    for j in range(2):
        STT("dve", cT[:, j, :], ckv_raw[:, j, :], pp[:, PP_KVG + j:PP_KVG + j + 1], rs[:], ALU.mult, ALU.mult,
            ["ckv_raw", "pp", "rs"], ["cT"])
    CP("dve", wukb[:], wstg[:], ["wstg"], ["wukb"])
    CP("pool", wuvb[:], wstg2[:], ["wstg2"], ["wuvb"])
    ki = kw[:, :, 0:64]
    P.op("dve", lambda e: e.tensor_reduce(m1[:], ki, AX.X, ALU.add), reads=["kw"], writes=["m1"])
    TS("dve", m1[:], m1[:], -1.0 / 64, None, ALU.mult, None, ["m1"], ["m1"])
    TT("dve", cen[:], ki, m1[:].to_broadcast([128, 16, 64]), ALU.add, ["kw", "m1"], ["cen"])
    TT("dve", sq2[:], cen[:], cen[:], ALU.mult, ["cen"], ["sq2"])
    P.op("dve", lambda e: e.tensor_reduce(v1[:], sq2[:], AX.X, ALU.add), reads=["sq2"], writes=["v1"])
    ACT(v1[:], v1[:], AF.Ln, ["v1", "epsT"], ["v1"], scale=1.0 / 64, bias=epsT[:])
    ACT(v1[:], v1[:], AF.Exp, ["v1"], ["v1"], scale=-0.5)
    TT("dve", cen[:], cen[:], v1[:].to_broadcast([128, 16, 64]), ALU.mult, ["cen", "v1"], ["cen"])
    gview = bcA[:, 0:64].rearrange("p (o c) -> p o c", o=1).to_broadcast([128, 16, 64])
    bview = bcA[:, 64:128].rearrange("p (o c) -> p o c", o=1).to_broadcast([128, 16, 64])
    TT("dve", cen[:], cen[:], gview, ALU.mult, ["cen", "bcA"], ["cen"])
    TT("dve", kdup[:, :, 0:64], cen[:], bview, ALU.add, ["cen", "bcA"], ["kdup"])
    CP("pool", kdup[:, :, 64:128], kdup[:, :, 0:64], ["kdup"], ["kdup"])
    for tb in range(16):
        b = 4 + tb // 4
        TR(bank(b, 128, (tb % 4) * 128), kdup[:, tb, :], ident, ["kdup", "cst"], b)
        if tb % 4 == 3:
            CP("act", kiT2[:, (tb // 4) * 512:(tb // 4 + 1) * 512], bank(b), [], [bname(b), "kiT2"])
    TS("pool", wi[:], kw[:, :, 64:80], 0.25, None, ALU.mult, None, ["kw"], ["wi"])
    P.barrier()
    if "cT" in dbg:
        dump("cT", cT[:], [128, 2, T], BF16, ["cT"])
        dump("kiT2", kiT2[:], [128, T], BF16, ["kiT2"])
        dump("wi", wi[:], [128, 16, 16], F32, ["wi"])
    if upto <= 1:
        return finish(nc, P, dbg_out, out_d)

    s_ = Bump(mem, 5 * KB, 146 * KB, (2, 2))
    qs = [s_(f"qs{i}", [128, 16, 128], BF16) for i in range(2)]
    dg = [s_(f"dg{i}", [128, 16, 128], BF16) for i in range(2)]
    NR = 6
    Rb = [s_(f"R{i}", [128, 512], BF16) for i in range(NR)]
    scores2 = [s_(f"scores{i}", [128, 7424], F32) for i in range(2)]
    junk = [s_(f"junk{i}", [128, T], BF16) for i in range(3)]
    jc = [0]
    mtm = [s_(f"mtm{i}", [128, T], BF16) for i in range(2)]
    lo_t = s_(f"lo_t{g}", [128, 4], F32)
    w0_t = s_("w0_t", [128, 4], F32)
    mid_t = s_("mid_t", [128, 4], F32)
    cnt_t = s_("cnt_t", [128, 4], F32)
    ge_t = s_("ge_t", [128, 4], F32)
    hi_t = s_("hi_t", [128, 4], F32)
    qiT_r = qiT_d.rearrange("hp (r d) t -> d (hp r) t", r=2)
    MS("pool", maskT[:], 0.0, ["maskT"])
    for i in range(2):
        MS("dve", qs[i][64:128, :, :], 0.0, [f"qs{i}"])
    rc = [0]
    scb = [0]
    accb = [0]
    trb = [0]
    LAG = 2

    QORD = [4 * g + j for g in (3, 2, 1, 0) for j in range(4)]
    QPOS = {qb: i for i, qb in enumerate(QORD)}

    def diag_build(qb):
        sl = QPOS[qb] % 2
        DMA("sp", qs[sl][0:64, :, :], qiT_r[:, :, qb * 128:(qb + 1) * 128], [], [f"qs{sl}"], f"qs{sl}")
        for h in range(16):
            ACT(dg[sl][:, h, :], identb[:], AF.Copy, ["identb", "wi"], [f"dg{sl}_{h}"], scale=wi[:, qb, h:h + 1])

    lo_g = [s_(f"lo_g{i}", [128, 4], F32) for i in range(4)]

    def goffs(g):
        offs = {}
        o_ = 0
        for qb in range(4 * g, 4 * g + 4):
            offs[qb] = o_
            o_ += 128 * (qb + 1)
        return offs

    def indexer(g):
        scores = scores2[g % 2]
        offs = {}
        o_ = 0
        for qb in range(4 * g, 4 * g + 4):
            offs[qb] = o_
            o_ += 128 * (qb + 1)
        for qb in range(4 * g, 4 * g + 4):
            sl = QPOS[qb] % 2
            L = 128 * (qb + 1)
            if QPOS[qb] + 1 < 16:
                diag_build(QORD[QPOS[qb] + 1])
            for kc in range((L + 511) // 512):
                n = min(512, L - 512 * kc)
                ab = 4 + accb[0] % 2
                accb[0] += 1
                pend = []
                for step in range(16 + LAG):
                    if step < 16:
                        h = step
                        sb_ = scb[0] % 4
                        scb[0] += 1
                        MM(bank(sb_, n), qs[sl][:, h, :], kiT2[:, kc * 512:kc * 512 + n],
                           True, True, [f"qs{sl}", "kiT2"], sb_)
                        ri = rc[0] % NR
                        rc[0] += 1
                        if g == 3 and h % 2 == 1:
                            TS("dve", Rb[ri][:, 0:n], bank(sb_, n), 0.0, None, ALU.max, None, [],
                               [bname(sb_), f"R{ri}"])
                        else:
                            ACT(Rb[ri][:, 0:n], bank(sb_, n), AF.Relu, [], [bname(sb_), f"R{ri}"])
                        pend.append((h, ri))
                    if step >= LAG:
                        h, ri = pend[step - LAG]
                        MM(bank(ab, n), dg[sl][:, h, :], Rb[ri][:, 0:n], h == 0, h == 15, [f"dg{sl}_{h}", f"R{ri}"], ab)
                base = offs[qb] + kc * 512
                last = (kc == (L + 511) // 512 - 1)
                CP("act", scores[:, base:base + n], bank(ab, n), [], [bname(ab), f"sc{(qb // 4) % 2}_{qb % 4}"])
                if last:
                    TT("pool", scores[:, base + n - 128:base + n], scores[:, base + n - 128:base + n], cnegTM, ALU.add,
                       ["cst", f"sc{(qb // 4) % 2}_{qb % 4}"], [f"sc{(qb // 4) % 2}_{qb % 4}"])
    def bisect(g):
        scores = scores2[g % 2]
        offs = goffs(g)
        lo_t = lo_g[g]
        qbs = list(range(4 * g, 4 * g + 4))
        scr_names = [f"sc{(qb // 4) % 2}_{qb % 4}" for qb in qbs]
        MS("dve", lo_t[:], -1e29, [f"lo_t{g}"])
        MS("dve", w0_t[:], 0.0, ["w0_t"])
        MS("dve", cnt_t[:], 0.0, ["cnt_t"])
        for j, qb in enumerate(qbs):
            if qb < 2:
                continue
            L = 128 * (qb + 1)
            sv = scores[:, offs[qb]:offs[qb] + L]
            P.op("dve", lambda e, sv=sv, j=j: e.tensor_reduce(hi_t[:, j:j + 1], sv, AX.X, ALU.max),
                 reads=[f"sc{(qb // 4) % 2}_{qb % 4}"], writes=["hi_t"])
            P.op("dve", lambda e, sv=sv, j=j: e.tensor_reduce(lo_t[:, j:j + 1], sv[:, 0:256], AX.X, ALU.min),
                 reads=[f"sc{(qb // 4) % 2}_{qb % 4}"], writes=[f"lo_t{g}"])
            TT("dve", w0_t[:, j:j + 1], hi_t[:, j:j + 1], lo_t[:, j:j + 1], ALU.subtract, ["hi_t", f"lo_t{g}"], ["w0_t"])
        if g > 0 or True:
            for k in range(NIT):
                c_ = 0.5 ** (k + 1)
                STT("dve", mid_t[:], w0_t[:], c_, lo_t[:], ALU.mult, ALU.add, ["w0_t", f"lo_t{g}"], ["mid_t"])
                for j, qb in enumerate(qbs):
                    if qb < 2:
                        continue
                    L = 128 * (qb + 1)
                    ji = jc[0] % 3
                    jc[0] += 1
                    TS("dve", junk[ji][:, 0:L], scores[:, offs[qb]:offs[qb] + L], mid_t[:, j:j + 1], None,
                       ALU.is_ge, ALU.add, [f"sc{(qb // 4) % 2}_{qb % 4}", "mid_t"], [f"junk{ji}", f"cnt_t{j}"],
                       accum=cnt_t[:, j:j + 1])
                TS("dve", ge_t[:], cnt_t[:], float(TOPK), None, ALU.is_ge, None,
                   ["cnt_t"] + [f"cnt_t{j}" for j in range(4)], ["ge_t"])
                TT("dve", ge_t[:], ge_t[:], w0_t[:], ALU.mult, ["ge_t", "w0_t"], ["ge_t"])
                STT("dve", lo_t[:], ge_t[:], c_, lo_t[:], ALU.mult, ALU.add, ["ge_t", f"lo_t{g}"], [f"lo_t{g}"])
    def masks(g):
        scores = scores2[g % 2]
        offs = goffs(g)
        lo_t = lo_g[g]
        qbs = list(range(4 * g, 4 * g + 4))
        for j, qb in enumerate(qbs):
            L = 128 * (qb + 1)
            ms = qb % 2
            TS("dve", mtm[ms][:, 0:L], scores[:, offs[qb]:offs[qb] + L], lo_t[:, j:j + 1], None,
               ALU.is_ge, None, [f"sc{(qb // 4) % 2}_{qb % 4}", f"lo_t{g}"], [f"mtm{ms}"])
            for kb in range(qb + 1):
                tb_ = 6 + trb[0] % 2
                trb[0] += 1
                pt = bank(tb_).bitcast(BF16)[:, 0:128]
                TR(pt, mtm[ms][:, kb * 128:(kb + 1) * 128], identb[:], [f"mtm{ms}", "identb"], tb_)
                dst = maskT[:, mt_off[kb] + (qb - kb) * 128: mt_off[kb] + (qb - kb + 1) * 128]
                CP("act" if trb[0] % 2 else "dve", dst, pt, ["maskT"], [bname(tb_), f"maskT_{kb}_{qb}"])
        if "thr" in dbg:
            dump(f"thr{g}", lo_t[:], [128, 4], F32, [f"lo_t{g}"])
    diag_build(QORD[0])
    indexer(3)
    bisect(3)
    for g in (2, 1, 0):
        indexer(g)
        masks(g + 1)
        bisect(g)
    masks(0)
    P.barrier()
    if "maskT" in dbg:
        dump("maskT", maskT[:], [128, MT_N], BF16, ["maskT"])
    if upto <= 2:
        return finish(nc, P, dbg_out, out_d)

    t_ = Bump(mem, 69 * KB, 156 * KB, (3, 3))
    Qh = [t_(f"Qh{i}", [128, T], BF16) for i in range(2)]
    KTh = [t_(f"KTh{i}", [128, T], BF16) for i in range(2)]
    Vh = [t_(f"Vh{i}", [128, 16, 129], BF16) for i in range(2)]
    gah = [t_(f"gah{i}", [128, 16, 128], BF16) for i in range(2)]
    NPT = 32
    PT = [t_(f"PT{i}", [128, 512], BF16) for i in range(NPT)]
    NE = 4
    Eb = [t_(f"E{i}", [128, 512], BF16) for i in range(NE)]
    bst = t_("bst", [128, 8, 2, 128], F32)
    bT = t_("bT", [128, 8, 2, 128], BF16)
    rden = [t_(f"rden{i}", [128, 1], F32) for i in range(2)]
    ya = [t_(f"ya{i}", [128, 128], BF16) for i in range(2)]
    DMA("sp", bst[:], bias_d.rearrange("h d p c -> p h d c"), [], ["bst"], "bst")
    for h in range(8):
        for dl in range(2):
            if dl == 0:
                STT("dve", bT[:, h, 0, :], bst[:, h, 0, :], pp[:, PP_B31 + h:PP_B31 + h + 1], cnegT,
                    ALU.subtract, ALU.add, ["bst", "pp", "cst"], ["bT"])
            else:
                TS("dve", bT[:, h, 1, :], bst[:, h, 1, :], pp[:, PP_B31 + h:PP_B31 + h + 1], None,
                   ALU.subtract, None, ["bst", "pp"], ["bT"])
    for i in range(2):
        MS("pool", Vh[i][:], 1.0, [f"Vh{i}"])
    ga_r = ga_d.rearrange("(tb p) n -> p tb n", p=128)
    ptc, ec, stc, pvc, yc = [0], [0], [0], [0], [0]
    pendq = []

    def attn_units(vres, Vt, vcols, weight_fn, post_fn, nst=4):
        def emit_pv(Tb, chunks):
            for j in range(4):
                qb = 4 * Tb + j
                pvb = 4 + pvc[0] % 2
                pvc[0] += 1
                for kb in range(qb + 1):
                    MM(bank(pvb, vcols), PT[chunks[kb]][:, j * 128:(j + 1) * 128], Vt[:, kb, :],
                       kb == 0, kb == qb, [f"PT{chunks[kb]}", vres], pvb)
                fin = post_fn(qb, pvb)
                if pendq:
                    pendq.pop(0)()
                pendq.append(fin)

        prev = None
        for Tb in range(4):
            nkb = 4 * Tb + 4
            chunks = {}
            for kb in range(nkb):
                c0 = max(Tb * 512, kb * 128)
                ncols = (Tb + 1) * 512 - c0
                lo_ = c0 - Tb * 512
                stb = stc[0] % nst
                stc[0] += 1
                pi = ptc[0] % NPT
                ptc[0] += 1
                chunks[kb] = pi
                weight_fn(kb, Tb, c0, ncols, lo_, stb, pi)
            if prev is not None:
                emit_pv(*prev)
            prev = (Tb, chunks)
        emit_pv(*prev)

    def prepT(h):
        hs = h % 2
        DMA("sp", Qh[hs][:], qaT_d[h], [], [f"Qh{hs}"], f"Qh{hs}")
        DMA("pool", gah[hs][:], ga_r[:, :, h * 128:(h + 1) * 128], [], [f"gah{hs}"], f"gah{hs}")
        for tq in range(4):
            for j in range(2):
                MM(bank(tq), wukb[:, j, h * 128:(h + 1) * 128], cT[:, j, tq * 512:(tq + 1) * 512],
                   j == 0, j == 1, ["wukb", "cT"], tq)
        CP("act", KTh[hs][:], PS[:, 0:2048], [], set_banks(0) + [f"KTh{hs}"])
        for tb4 in range(4):
            for jj in range(4):
                tb = tb4 * 4 + jj
                for j in range(2):
                    MM(bank(tb4, 128, jj * 128), cT[:, j, tb * 128:(tb + 1) * 128], wuvb[:, j, h * 128:(h + 1) * 128],
                       j == 0, j == 1, ["cT", "wuvb"], tb4)
            CP("dve", Vh[hs][:, tb4 * 4:(tb4 + 1) * 4, 0:128], bank(tb4).rearrange("p (a c) -> p a c", a=4),
               [], [bname(tb4), f"Vh{hs}"])

    prepT(0)
    for h in range(8):
        hs = h % 2
        if h + 1 < 8:
            prepT(h + 1)

        def wfn(kb, Tb, c0, ncols, lo_, stb, pi, h=h, hs=hs):
            dls = [dl for dl in (0, 1) if 4 * Tb <= kb + dl < 4 * Tb + 4]
            MM(bank(stb, ncols, lo_), KTh[hs][:, kb * 128:(kb + 1) * 128], Qh[hs][:, c0:c0 + ncols],
               True, len(dls) == 0, [f"KTh{hs}", f"Qh{hs}"], stb)
            for ii, dl in enumerate(dls):
                jj = kb + dl - 4 * Tb
                MM(bank(stb, 128, jj * 128), identb[:], bT[:, h, dl, :], False, ii == len(dls) - 1,
                   ["identb", "bT"], stb)
            ei = ec[0] % NE
            ec[0] += 1
            ACT(Eb[ei][:, lo_:512], bank(stb, ncols, lo_), AF.Exp, [], [bname(stb), f"E{ei}"])
            mo = mt_off[kb] + (c0 - 128 * kb)
            TT("dve", PT[pi][:, lo_:512], Eb[ei][:, lo_:512], maskT[:, mo:mo + ncols], ALU.mult,
               [f"E{ei}", "maskT"], [f"PT{pi}"])

        def pfn(qb, pvb, h=h, hs=hs):
            yi = yc[0] % 2
            yc[0] += 1
            P.op("dve", lambda e: e.reciprocal(rden[yi][:], bank(pvb, 1, 128)), writes=[bname(pvb), f"rden{yi}"])
            STT("dve", ya[yi][:], bank(pvb, 128), rden[yi][:], gah[hs][:, qb, :], ALU.mult, ALU.mult,
                [f"rden{yi}", f"gah{hs}"], [bname(pvb), f"ya{yi}"])
            def fin():
                tb_ = 6 + yi
                pt = bank(tb_).bitcast(BF16)[:, 0:128]
                TR(pt, ya[yi][:], identb[:], [f"ya{yi}", "identb"], tb_)
                CP("act", yT[:, h, qb * 128:(qb + 1) * 128], pt, [], [bname(tb_), "yT"])
            return fin

        attn_units(f"Vh{hs}", Vh[hs], 129, wfn, pfn)
    while pendq:
        pendq.pop(0)()
    P.barrier()
    if "yTa" in dbg:
        dump("yTa", yT[:, 0:8, :], [128, 8, T], BF16, ["yT"])
    if upto <= 3:
        return finish(nc, P, dbg_out, out_d)

    m_ = Bump(mem, 69 * KB, LIMIT, (4, 4))
    bcM = m_("bcM", [128, 1032], F32)
    fpre = m_("fpre", [128, 16, 4], F32)
    ipre = m_("ipre", [128, 16, 4], F32)
    lfn = m_("lfn", [128, 16, 4], F32)
    lfx = m_("lfx", [128, 16, 4], F32)
    u_t = m_("u_t", [128, 16, 4], F32)
    rt = [m_(f"rt{i}", [128, 128], F32) for i in range(4)]
    qmh = [m_(f"qmh{i}", [128, T], BF16) for i in range(2)]
    kmh = [m_(f"kmh{i}", [128, T], BF16) for i in range(2)]
    Vmh = [m_(f"Vmh{i}", [128, 16, 257], BF16) for i in range(2)]
    gomh = m_("gomh", [128, 16, 256], BF16)
    gzmh = m_("gzmh", [128, 16, 256], BF16)
    gall = [m_(f"gall{i}", [128, 16, 256], BF16) for i in range(2)]
    Fbc = [m_(f"Fbc{i}", [128, T], F32) for i in range(2)]
    NW = 4
    Wb = [m_(f"W{i}", [128, 512], F32) for i in range(NW)]
    PT = [m_(f"PTm{i}", [128, 512], BF16) for i in range(NPT)]
    NPB = 3
    nsb = [m_(f"nsb{i}", [128, 257], F32) for i in range(NPB)]
    hn = [m_(f"hn{i}", [128, 256], F32) for i in range(NPB)]
    ym = [m_(f"ym{i}", [128, 256], BF16) for i in range(NPB)]
    st6 = [m_(f"st6{i}", [128, 6], F32) for i in range(NPB)]
    mv = [m_(f"mv{i}", [128, 2], F32) for i in range(NPB)]
    ddt = [m_(f"dd{i}", [128, 1], F32) for i in range(NPB)]
    t1t = [m_(f"t1{i}", [128, 1], F32) for i in range(NPB)]
    DMA("sp", bcM[:], bc_d[:, BC_BI:BC_BI + 1032], [], ["bcM"], "bcM")
    for i in range(2):
        MS("pool", Vmh[i][:], 1.0, [f"Vmh{i}"])
    bi_v = bcM[:, 0:4].rearrange("p (o c) -> p o c", o=1).to_broadcast([128, 16, 4])
    bf_v = bcM[:, 4:8].rearrange("p (o c) -> p o c", o=1).to_broadcast([128, 16, 4])
    TT("dve", ipre[:], ifg[:, :, 0:4], bi_v, ALU.add, ["ifg", "bcM"], ["ipre"])
    TT("dve", fpre[:], ifg[:, :, 4:8], bf_v, ALU.add, ["ifg", "bcM"], ["fpre"])
    ACT(lfn[:], fpre[:], AF.Exp, ["fpre"], ["lfn"], scale=-1.0)
    ACT(lfn[:], lfn[:], AF.Ln, ["lfn"], ["lfn"], bias=1.0)
    MS("dve", lfx[:, 0, :], 0.0, ["lfx"])
    for tb in range(1, 16):
        TT("dve", lfx[:, tb, :], lfx[:, tb - 1, :], lfn[:, tb - 1, :], ALU.add, ["lfx", "lfn"], ["lfx"])
    MM(bank(0, 64), tri, lfn[:].rearrange("p a b -> p (a b)"), True, False, ["cst", "lfn"], 0)
    MM(bank(0, 64), ones, lfx[:].rearrange("p a b -> p (a b)"), False, True, ["cst", "lfx"], 0)
    STT("dve", u_t[:].rearrange("p a b -> p (a b)"), bank(0, 64), math.log(128.0 ** -0.5),
        ipre[:].rearrange("p a b -> p (a b)"), ALU.add, ALU.add, ["ipre"], [bname(0), "u_t"])
    vm_r = vm_d.rearrange("(tb p) n -> p tb n", p=128)
    gom_r = gom_d.rearrange("(tb p) n -> p tb n", p=128)
    gzm_r = gzm_d.rearrange("(tb p) n -> p tb n", p=128)
    wc, rtc = [0], [0]
    def prepM(h):
        hs = h % 2
        DMA("sp", qmh[hs][:], qkT_d[h], [], [f"qmh{hs}"], f"qmh{hs}")
        DMA("pool", kmh[hs][:], qkT_d[4 + h], [], [f"kmh{hs}"], f"kmh{hs}")
        DMA("sp", Vmh[hs][:, :, 0:256], vm_r[:, :, h * 256:(h + 1) * 256], [], [f"Vmh{hs}"], f"Vmh{hs}")
        DMA("pool", gomh[:], gom_r[:, :, h * 256:(h + 1) * 256], [], ["gomh"], "gomh")
        DMA("sp", gzmh[:], gzm_r[:, :, h * 256:(h + 1) * 256], [], ["gzmh"], "gzmh")
        TT("pool", gall[hs][:], gomh[:], gzmh[:], ALU.mult, ["gomh", "gzmh"], [f"gall{hs}"])
        mg = bcM[:, 8 + h * 256: 8 + (h + 1) * 256].rearrange("p (o c) -> p o c", o=1).to_broadcast([128, 16, 256])
        TT("pool", gall[hs][:], gall[hs][:], mg, ALU.mult, [f"gall{hs}", "bcM"], [f"gall{hs}"])
        for tb in range(16):
            ri = rtc[0] % 4
            rtc[0] += 1
            TS("dve", rt[ri][:], tri, lfn[:, tb, h:h + 1], lfx[:, tb, h:h + 1], ALU.mult, ALU.add,
               ["cst", "lfn", "lfx"], [f"rt{ri}"])
            b = tb // 4
            MM(bank(b, 128, (tb % 4) * 128), ones, rt[ri][:], True, True, ["cst", f"rt{ri}"], b)
            if tb % 4 == 3:
                ACT(Fbc[hs][:, b * 512:(b + 1) * 512], bank(b), AF.Copy, [], [bname(b), f"Fbc{hs}"], scale=-1.0)

    prepM(0)
    for h in range(4):
        hs = h % 2
        if h + 1 < 4:
            prepM(h + 1)

        def wfn(kb, Tb, c0, ncols, lo_, stb, pi, h=h, hs=hs):
            wi_ = wc[0] % NW
            wc[0] += 1
            ACT(Wb[wi_][:, lo_:512], Fbc[hs][:, c0:c0 + ncols], AF.Exp, [f"Fbc{hs}", "u_t"], [f"W{wi_}"],
                bias=u_t[:, kb, h:h + 1])
            MM(bank(stb, ncols, lo_), kmh[hs][:, kb * 128:(kb + 1) * 128], qmh[hs][:, c0:c0 + ncols],
               True, True, [f"kmh{hs}", f"qmh{hs}"], stb)
            TT("dve", PT[pi][:, lo_:512], bank(stb, ncols, lo_), Wb[wi_][:, lo_:512], ALU.mult,
               [f"W{wi_}"], [bname(stb), f"PT{pi}"])
            if kb >= 4 * Tb:
                TT("dve", PT[pi][:, lo_:lo_ + 128], PT[pi][:, lo_:lo_ + 128], c01Tb[:], ALU.mult,
                   [f"PT{pi}", "c01Tb"], [f"PT{pi}"])

        def pfn(qb, pvb, h=h, hs=hs):
            yi = yc[0] % NPB
            yc[0] += 1
            CP("act", nsb[yi][:], bank(pvb, 257), [], [bname(pvb), f"nsb{yi}"])
            Nap = nsb[yi][:, 0:256]
            nr = [f"nsb{yi}"]
            P.op("dve", lambda e: e.bn_stats(st6[yi][:], Nap), reads=nr, writes=[f"st6{yi}"])
            P.op("dve", lambda e: e.bn_aggr(mv[yi][:], st6[yi][:]), reads=[f"st6{yi}"], writes=[f"mv{yi}"])
            TT("dve", t1t[yi][:], nsb[yi][:, 256:257], nsb[yi][:, 256:257], ALU.mult, nr, [f"t1{yi}"])
            TS("dve", t1t[yi][:], t1t[yi][:], 1.0, LN_EPS, ALU.max, ALU.mult, [f"t1{yi}"], [f"t1{yi}"])
            TT("dve", t1t[yi][:], t1t[yi][:], mv[yi][:, 1:2], ALU.add, [f"t1{yi}", f"mv{yi}"], [f"t1{yi}"])
            ACT(t1t[yi][:], t1t[yi][:], AF.Ln, [f"t1{yi}"], [f"t1{yi}"])
            ACT(t1t[yi][:], t1t[yi][:], AF.Exp, [f"t1{yi}"], [f"t1{yi}"], scale=-0.5)
            TS("dve", hn[yi][:], Nap, mv[yi][:, 0:1], t1t[yi][:], ALU.subtract, ALU.mult,
               nr + [f"mv{yi}", f"t1{yi}"], [f"hn{yi}"])
            TT("dve", ym[yi][:], hn[yi][:], gall[hs][:, qb, :], ALU.mult, [f"hn{yi}", f"gall{hs}"], [f"ym{yi}"])
            def fin():
                tb_ = 6 + yi % 2
                pt = bank(tb_).bitcast(BF16)[:, 0:256]
                for i2 in range(2):
                    TR(pt[:, i2 * 128:(i2 + 1) * 128], ym[yi][:, i2 * 128:(i2 + 1) * 128], identb[:],
                       [f"ym{yi}", "identb"], tb_)
                CP("act", yT[:, 8 + 2 * h:10 + 2 * h, qb * 128:(qb + 1) * 128],
                   pt.rearrange("p (a c) -> p a c", a=2), [], [bname(tb_), "yT"])
            return fin

        attn_units(f"Vmh{hs}", Vmh[hs], 257, wfn, pfn)
    while pendq:
        pendq.pop(0)()
    P.barrier()
    if "yTm" in dbg:
        dump("yTm", yT[:, 8:16, :], [128, 8, T], BF16, ["yT"])
    if upto <= 4:
        return finish(nc, P, dbg_out, out_d)

    o_ = Bump(mem, 69 * KB, LIMIT, (5, 5))
    wo_bf = o_("wo_bf", [128, 16, D], BF16)
    wstO = [o_(f"wstO{i}", [128, 16, 256], F32) for i in range(2)]
    xres = [o_(f"xres{i}", [128, D], F32) for i in range(3)]
    lngb = o_("lngb", [128, 2 * D], F32)
    stO = [o_(f"stO{i}", [128, 4, 6], F32) for i in range(3)]
    mvO = [o_(f"mvO{i}", [128, 2], F32) for i in range(3)]
    rsO = [o_(f"rsO{i}", [128, 1], F32) for i in range(3)]
    wout_r = wout_d.rearrange("(fc p) n -> p fc n", p=128)
    DMA("pool", lngb[:], bc_d[:, BC_LNG:BC_LNG + 2 * D], [], ["lngb"], "lngb")
    for pc in range(8):
        s = pc % 2
        DMA("sp", wstO[s][:, 0:8, :], wout_r[:, 0:8, pc * 256:(pc + 1) * 256], [], [f"wstO{s}a"], f"wstO{s}a")
        DMA("pool", wstO[s][:, 8:16, :], wout_r[:, 8:16, pc * 256:(pc + 1) * 256], [], [f"wstO{s}b"], f"wstO{s}b")
        CP(("dve", "act")[pc % 2], wo_bf[:, :, pc * 256:(pc + 1) * 256], wstO[s][:],
           [f"wstO{s}a", f"wstO{s}b"], ["wo_bf"])
    for tb in range(16):
        s = tb % 3
        DMA("sp", xres[s][:], x_d[tb * 128:(tb + 1) * 128, :], [], [f"xres{s}"], f"xres{s}")
        st = next_set()
        for n4 in range(4):
            for fc in range(16):
                MM(bank(st * 4 + n4), yT[:, fc, tb * 128:(tb + 1) * 128], wo_bf[:, fc, n4 * 512:(n4 + 1) * 512],
                   fc == 0, fc == 15, ["yT", "wo_bf"], st * 4 + n4)
        STT("dve", xres[s][:], xres[s][:], ALPHA, set_ap(st), ALU.mult, ALU.add, [f"xres{s}"],
            set_banks(st) + [f"xres{s}"])
        for c4 in range(4):
            P.op("dve", lambda e, c4=c4, s=s: e.bn_stats(stO[s][:, c4, :], xres[s][:, c4 * 512:(c4 + 1) * 512]),
                 reads=[f"xres{s}"], writes=[f"stO{s}"])
        P.op("dve", lambda e, s=s: e.bn_aggr(mvO[s][:], stO[s][:].rearrange("p a b -> p (a b)")),
             reads=[f"stO{s}"], writes=[f"mvO{s}"])
        ACT(rsO[s][:], mvO[s][:, 1:2], AF.Ln, [f"mvO{s}", "epsT"], [f"rsO{s}"], bias=epsT[:])
        ACT(rsO[s][:], rsO[s][:], AF.Exp, [f"rsO{s}"], [f"rsO{s}"], scale=-0.5)
        TS("dve", xres[s][:], xres[s][:], mvO[s][:, 0:1], rsO[s][:], ALU.subtract, ALU.mult,
           [f"xres{s}", f"mvO{s}", f"rsO{s}"], [f"xres{s}"])
        TT("dve", xres[s][:], xres[s][:], lngb[:, 0:D], ALU.mult, [f"xres{s}", "lngb"], [f"xres{s}"])
        TT("pool", xres[s][:], xres[s][:], lngb[:, D:2 * D], ALU.add, [f"xres{s}", "lngb"], [f"xres{s}"])
        DMA("sp", out_d[tb * 128:(tb + 1) * 128, :], xres[s][:], [f"xres{s}"], [f"out{tb}"], f"xres{s}")
    return finish(nc, P, dbg_out, out_d)


def finish(nc, P, dbg_out, out_d):
    P.barrier()
    with ExitStack() as st:
        P.emit(st)
    return nc, dbg_out


def _t5_bucket(n):
    n = np.maximum(n, 0)
    nf = np.maximum(n, 1).astype(np.float32)
    large = 16 + (np.log(nf / 16) / math.log(128 / 16) * 16).astype(np.int32)
    large = np.minimum(large, 31)
    return np.where(n < 16, n, large)


def _host_inputs(inp):
    f = lambda a: np.ascontiguousarray(np.asarray(a, dtype=np.float32))
    w_ukT = f(np.asarray(inp["w_uk"])[0].transpose(2, 0, 1).reshape(256, 1024))
    w_uv2 = f(np.asarray(inp["w_uv"])[0].transpose(1, 0, 2).reshape(256, 1024))
    rel_bias = np.asarray(inp["rel_bias"], dtype=np.float32)
    p = np.arange(128)[:, None]
    c = np.arange(128)[None, :]
    biasT = np.zeros((8, 2, 128, 128), np.float32)
    for dl in range(2):
        nrel = 128 * dl + c - p
        g = rel_bias[_t5_bucket(nrel)]
        g = np.where((nrel >= 0)[..., None], g, np.float32(0))
        biasT[:, dl] = g.transpose(2, 0, 1)
    pp = np.zeros((128, PP_N), np.float32)
    pp[:, PP_KVG:PP_KVG + 2] = np.asarray(inp["kv_norm_g"])[0].reshape(2, 128).T
    cw = np.asarray(inp["conv_w"])[0]
    pp[:, PP_CONVW:PP_CONVW + 32] = cw.reshape(4, 8, 128).transpose(2, 1, 0).reshape(128, 32)
    pp[:, PP_CONVB:PP_CONVB + 8] = np.asarray(inp["conv_b"])[0].reshape(8, 128).T
    pp[:, PP_B31:PP_B31 + 8] = np.broadcast_to(rel_bias[31][None, :], (128, 8))
    bc = np.zeros((128, BC_N), np.float32)
    rep = lambda v: np.broadcast_to(np.asarray(v, dtype=np.float32).reshape(1, -1), (128, np.asarray(v).size))
    bc[:, BC_IDXG:BC_IDXG + 64] = rep(np.asarray(inp["idx_k_ln_g"])[0])
    bc[:, BC_IDXB:BC_IDXB + 64] = rep(np.asarray(inp["idx_k_ln_b"])[0])
    bc[:, BC_BI:BC_BI + 4] = rep(np.asarray(inp["b_igate"])[0])
    bc[:, BC_BF:BC_BF + 4] = rep(np.asarray(inp["b_fgate"])[0])
    bc[:, BC_MHG:BC_MHG + 1024] = rep(np.asarray(inp["mh_norm_g"])[0])
    bc[:, BC_LNG:BC_LNG + 2048] = rep(np.asarray(inp["ln_g"])[0])
    bc[:, BC_LNB:BC_LNB + 2048] = rep(np.asarray(inp["ln_b"])[0])
    cst = np.zeros((128, 5, 128), np.float32)
    cst[:, 0] = np.eye(128)
    cst[:, 1] = 1.0
    cst[:, 2] = (p <= c)
    cst[:, 3] = np.where(c <= p, 0.0, -1e30)
    cst[:, 4] = np.where(c >= p, 0.0, -30000.0)
    shared = dict(w_in=f(np.asarray(inp["w_in"])[0]), w_out=f(np.asarray(inp["w_out"])[0]),
                  w_ukT=w_ukT, w_uv2=w_uv2, biasT=biasT, pp=pp, bc=bc, cst=cst)
    x = np.asarray(inp["x"], dtype=np.float32)
    return [dict(shared, x=np.ascontiguousarray(x[i])) for i in range(8)]


def kernel(**inputs):
    nc, _ = build_program()
    in_maps = _host_inputs(inputs)
    res = run_bass_kernel_spmd(nc, in_maps, core_ids=list(range(8)))
    return np.stack([np.asarray(r["out"], dtype=np.float32) for r in res.results], axis=0)
```

```python
import math
from contextlib import ExitStack

import numpy as np
import ml_dtypes
import concourse.bass as bass
import concourse.mybir as mybir
from concourse.bass_utils import run_bass_kernel_spmd

F32 = mybir.dt.float32
BF16 = mybir.dt.bfloat16
ALU = mybir.AluOpType
AF = mybir.ActivationFunctionType
AX = mybir.AxisListType

T = 2048
D = 2048
NCOLS = 7512
NIT = 15
TOPK = 256
LN_EPS = 1e-5
ALPHA = 2.0 ** 0.25
B0 = 16640
LIMIT = 212000

ENGS = ("pe", "act", "dve", "pool", "sp")


class Prog:
    def __init__(self, nc):
        self.nc = nc
        self.ops = {e: [] for e in ENGS}
        self.last_w = {}
        self.readers = {}
        self.slot_cnt = {}
        self.slots = []

    def op(self, eng, fn, reads=(), writes=(), slot=None, preads=()):
        isbank = lambda w: w.startswith("ps") and w[2:].isdigit()
        if eng != "pe":
            preads = list(preads) + [w for w in writes if isbank(w)]
            writes = [w for w in writes if not isbank(w)]
        deps = set()
        for r in reads:
            if r in self.last_w:
                deps.add(self.last_w[r])
        for r in preads:
            if r in self.last_w:
                deps.add(self.last_w[r])
            for rd in self.readers.get(r, ()):
                if rd[0] != eng:
                    deps.add(rd)
        for w in writes:
            if w in self.last_w:
                deps.add(self.last_w[w])
            for rd in self.readers.get(w, ()):
                deps.add(rd)
        is_dma = slot is not None
        if not is_dma and eng == "pe":
            deps = {d for d in deps if d[0] != "pe"}
        rec = dict(fn=fn, deps=deps, signal=False, dma=is_dma, slot=slot, sigval=None)
        if is_dma:
            if slot not in self.slot_cnt:
                self.slot_cnt[slot] = 0
                self.slots.append(slot)
            self.slot_cnt[slot] += 1
            rec["sigval"] = 16 * self.slot_cnt[slot]
        me = (eng, len(self.ops[eng]))
        self.ops[eng].append(rec)
        for w in writes:
            self.last_w[w] = me
            self.readers[w] = []
        for r in list(reads) + list(preads):
            if r not in writes:
                self.readers.setdefault(r, []).append(me)
        return me

    def barrier(self):
        lasts = set()
        for e in ENGS:
            for i in range(len(self.ops[e]) - 1, -1, -1):
                r = self.ops[e][i]
                if not r["dma"] and r["fn"] is not None:
                    lasts.add((e, i)); break
        dmas = set()
        for e in ENGS:
            seen = set()
            for i in range(len(self.ops[e]) - 1, -1, -1):
                r = self.ops[e][i]
                if r["dma"] and r["slot"] not in seen:
                    seen.add(r["slot"]); dmas.add((e, i))
        for e in ENGS:
            deps = {d for d in (lasts | dmas) if d[0] != e or self.ops[d[0]][d[1]]["dma"]}
            self.ops[e].append(dict(fn=None, deps=deps, signal=False, dma=False, slot=None, sigval=None))

    def emit(self, stack):
        nc = self.nc
        for e in ENGS:
            for rec in self.ops[e]:
                for (de, di) in rec["deps"]:
                    self.ops[de][di]["signal"] = True
        sem_e = {e: stack.enter_context(nc.semaphore("s_" + e)) for e in ENGS}
        sem_slot = {s: stack.enter_context(nc.semaphore("d_" + s)) for s in self.slots}
        for e in ENGS:
            c = 0
            for rec in self.ops[e]:
                if rec["dma"] or rec["fn"] is None:
                    continue
                if rec["signal"]:
                    c += 1
                    rec["sigval"] = c
        block = stack.enter_context(nc.Block())
        engobj = {"pe": block.tensor, "act": block.scalar, "dve": block.vector,
                  "pool": block.gpsimd, "sp": block.sync}
        for e in ENGS:
            ops = self.ops[e]

            def body(eng, ops=ops, e=e):
                waited = {}
                for rec in ops:
                    need = {}
                    for (de, di) in rec["deps"]:
                        d = self.ops[de][di]
                        if d["dma"]:
                            key = ("slot", d["slot"]); sem = sem_slot[d["slot"]]
                        else:
                            if d["sigval"] is None:
                                continue
                            key = ("eng", de); sem = sem_e[de]
                        v = d["sigval"]
                        if v > need.get(key, (None, 0))[1]:
                            need[key] = (sem, v)
                    for key, (sem, v) in need.items():
                        if waited.get(key, 0) >= v:
                            continue
                        eng.wait_ge(sem, v)
                        waited[key] = v
                    if rec["fn"] is None:
                        continue
                    ins = rec["fn"](eng)
                    if rec["dma"]:
                        ins.then_inc(sem_slot[rec["slot"]], 16)
                    elif rec["signal"]:
                        ins.then_inc(sem_e[e], 1)

            engobj[e](body)


class Mem:
    def __init__(self, nc):
        self.nc = nc
        self.allocs = []

    def alloc(self, name, shape, dtype, off, life):
        esz = 4 if dtype == F32 else 2
        size = esz * int(np.prod(shape[1:]))
        assert off % 32 == 0, (name, off)
        assert off >= 0 and off + size <= LIMIT, (name, off, size)
        for (n2, o2, s2, l2) in self.allocs:
            if l2[0] <= life[1] and life[0] <= l2[1] and o2 < off + size and off < o2 + s2:
                raise AssertionError(f"SBUF overlap {name} vs {n2}")
        self.allocs.append((name, off, size, life))
        return self.nc.alloc_sbuf_tensor_at(name, list(shape), dtype, offset=B0 + off)


class Bump:
    def __init__(self, mem, start, end, life):
        self.mem, self.cur, self.end, self.life = mem, start, end, life

    def __call__(self, name, shape, dtype):
        esz = 4 if dtype == F32 else 2
        size = esz * int(np.prod(shape[1:]))
        size = (size + 63) // 64 * 64
        off = self.cur
        assert off + size <= self.end, (name, off, size, self.end)
        self.cur += size
        return self.mem.alloc(name, shape, dtype, off, self.life)


KB = 1024
SEGS = [
    ("ckv", 1024, 256, "FM"), ("kiw", 3328, 80, "TM"),
    ("qi0", 2304, 512, "FM"), ("qi1", 2816, 512, "FM"),
    ("qa0", 0, 512, "FM"), ("qa1", 512, 512, "FM"),
    ("za0", 1280, 512, "TM"), ("za1", 1792, 512, "TM"),
    ("qm", 3408, 512, "FM"), ("km", 3920, 512, "FM"),
    ("if", 5456, 8, "TM"),
    ("vm0", 4432, 512, "TM"), ("vm1", 4944, 512, "TM"),
    ("om0", 5464, 512, "TM"), ("om1", 5976, 512, "TM"),
    ("zm0", 6488, 512, "TM"), ("zm1", 7000, 512, "TM"),
]
PP_KVG, PP_CONVW, PP_CONVB, PP_B31 = 0, 2, 34, 42
PP_N = 64
BC_IDXG, BC_IDXB, BC_BI, BC_BF, BC_MHG, BC_LNG, BC_LNB = 0, 64, 128, 132, 136, 1160, 3208
BC_N = 5256


def build_program(upto=9, dbg=()):
    nc = bass.Bass("TRN2", target_bir_lowering=False)
    dt_in = lambda name, shape, dt=F32: nc.dram_tensor(name, list(shape), dt, kind="ExternalInput").ap()
    x_d = dt_in("x", [T, D])
    win_d = dt_in("w_in", [D, NCOLS])
    wout_d = dt_in("w_out", [D, D])
    wuk_d = dt_in("w_ukT", [256, 1024])
    wuv_d = dt_in("w_uv2", [256, 1024])
    bias_d = dt_in("biasT", [8, 2, 128, 128])
    pp_d = dt_in("pp", [128, PP_N])
    bc_d = dt_in("bc", [128, BC_N])
    cst_d = dt_in("cst", [128, 5, 128])
    out_d = nc.dram_tensor("out", [T, D], F32, kind="ExternalOutput").ap()
    scr = lambda name, shape, dt=BF16: nc.dram_tensor(name, list(shape), dt, kind="Internal").ap()
    qaT_d = scr("qaT_d", [8, 128, T])
    qiT_d = scr("qiT_d", [8, 128, T])
    qkT_d = scr("qkT_d", [8, 128, T])
    ga_d = scr("ga_d", [T, 1024])
    vm_d = scr("vm_d", [T, 1024])
    gom_d = scr("gom_d", [T, 1024])
    gzm_d = scr("gzm_d", [T, 1024])
    dbg_out = {}

    P = Prog(nc)
    mem = Mem(nc)
    PS = nc.alloc_psum_tensor("PS", [128, 4096], F32)
    bank = lambda b, n=512, o=0: PS[:, b * 512 + o: b * 512 + o + n]
    bname = lambda b: f"ps{b}"

    def DMA(q, out, in_, reads, writes, slot):
        P.op(q, lambda e: e.dma_start(out=out, in_=in_), reads=reads, writes=writes, slot=slot)

    def MM(out, lhsT, rhs, start, stop, reads, b):
        P.op("pe", lambda e: e.matmul(out, lhsT, rhs, start=start, stop=stop), reads=reads, writes=[bname(b)])

    def TR(out, in_, ident, reads, b):
        P.op("pe", lambda e: e.transpose(out, in_, ident), reads=reads, writes=[bname(b)])

    def ACT(out, in_, func, reads, writes, scale=1.0, bias=0.0, eng="act"):
        P.op(eng, lambda e: e.activation(out=out, in_=in_, func=func, bias=bias, scale=scale),
             reads=reads, writes=writes)

    def TS(eng, out, in0, s1, s2, op0, op1, reads, writes, accum=None):
        if op1 is None:
            P.op(eng, lambda e: e.tensor_scalar(out, in0, s1, None, op0), reads=reads, writes=writes)
        elif accum is None:
            P.op(eng, lambda e: e.tensor_scalar(out, in0, s1, s2, op0, op1), reads=reads, writes=writes)
        else:
            P.op(eng, lambda e: e.tensor_scalar(out, in0, s1, s2, op0, op1, accum_out=accum),
                 reads=reads, writes=writes)

    def STT(eng, out, in0, s, in1, op0, op1, reads, writes):
        P.op(eng, lambda e: e.scalar_tensor_tensor(out, in0, s, in1, op0, op1), reads=reads, writes=writes)

    def TT(eng, out, in0, in1, op, reads, writes):
        P.op(eng, lambda e: e.tensor_tensor(out, in0, in1, op), reads=reads, writes=writes)

    def CP(eng, out, in_, reads, writes):
        if eng == "act":
            ACT(out, in_, AF.Copy, reads, writes)
        else:
            P.op(eng, lambda e: e.tensor_copy(out, in_), reads=reads, writes=writes)

    def MS(eng, ap, val, writes):
        P.op(eng, lambda e: e.memset(ap, val), writes=writes)

    def dump(name, src_ap, shape, dtype, reads):
        d = nc.dram_tensor("dbg_" + name, list(shape), dtype, kind="ExternalOutput").ap()
        dbg_out[name] = d
        DMA("sp", d, src_ap, reads, ["dbg_" + name], "dbg_" + name)

    ALL = (0, 5)
    cst = mem.alloc("cst", [128, 5, 128], F32, 0, ALL)
    identb = mem.alloc("identb", [128, 128], BF16, 2560, ALL)
    c01T = mem.alloc("c01T", [128, 128], F32, 2816, ALL)
    pp = mem.alloc("pp", [128, PP_N], F32, 3328, ALL)
    ifg = mem.alloc("ifg", [128, 16, 8], F32, 3584, (0, 4))
    ident, ones, tri, cnegTM, cnegT = (cst[:, i, :] for i in range(5))
    epsT = mem.alloc("epsT", [128, 1], F32, 4096, ALL)
    c01Tb = mem.alloc("c01Tb", [128, 128], BF16, 4096 + 64, ALL)
    kw = mem.alloc("kw", [128, 16, 80], F32, 146 * KB, (0, 2))
    ckv_raw = mem.alloc("ckv_raw", [128, 2, T], F32, 130 * KB, (0, 1))
    kiT2 = mem.alloc("kiT2", [128, T], BF16, 151 * KB, (1, 2))
    wi = mem.alloc("wi", [128, 16, 16], F32, 155 * KB, (1, 2))
    wukb = mem.alloc("wukb", [128, 2, 1024], BF16, 156 * KB, (1, 3))
    wuvb = mem.alloc("wuvb", [128, 2, 1024], BF16, 160 * KB, (1, 3))
    cT = mem.alloc("cT", [128, 2, T], BF16, 164 * KB, (1, 3))
    MT_N = 17408
    maskT = mem.alloc("maskT", [128, MT_N], BF16, 172 * KB, (2, 3))
    yT = mem.alloc("yT", [128, 16, T], BF16, 5 * KB, (3, 5))
    mt_off = [sum(T - 128 * k for k in range(kb)) for kb in range(17)]

    DMA("sp", cst[:], cst_d, [], ["cst"], "cst")
    DMA("sp", pp[:], pp_d, [], ["pp"], "pp")
    CP("dve", identb[:], ident, ["cst"], ["identb"])
    TS("dve", c01T[:], cnegT, 0.0, None, ALU.is_ge, None, ["cst"], ["c01T"])
    MS("pool", epsT[:], LN_EPS, ["epsT"])
    CP("dve", c01Tb[:], c01T[:], ["c01T"], ["c01Tb"])

    lo = Bump(mem, 5 * KB, 130 * KB, (0, 0))
    hi = Bump(mem, 151 * KB, LIMIT, (0, 0))
    xT = lo("xT", [128, 16, T], BF16)
    wst = lo("wst", [128, 16, 512], F32)
    xin = [lo(f"xin{i}", [128, D], F32) for i in range(2)]
    ev = [lo(f"ev{i}", [128, 2048], BF16) for i in range(2)]
    wbf = [hi(f"wbf{i}", [128, 16, 512], BF16) for i in range(2)]
    cin = hi("cin", [128, T + 3], F32)
    cacc = hi("cacc", [128, T], F32)
    win_r = win_d.rearrange("(dc p) n -> p dc n", p=128)

    def load_w(i):
        _, c0, n, _ = SEGS[i]
        DMA("sp", wst[:, 0:8, 0:n], win_r[:, 0:8, c0:c0 + n], [], ["wstA"], "wstA")
        DMA("pool", wst[:, 8:16, 0:n], win_r[:, 8:16, c0:c0 + n], [], ["wstB"], "wstB")

    load_w(0)
    for tb in range(16):
        s = tb % 2
        DMA("sp", xin[s][:], x_d[tb * 128:(tb + 1) * 128, :], [], [f"xin{s}"], f"xin{s}")
        for g in range(4):
            b = (tb * 4 + g) % 8
            for j in range(4):
                dc = g * 4 + j
                TR(bank(b, 128, j * 128), xin[s][:, dc * 128:(dc + 1) * 128], ident, [f"xin{s}", "cst"], b)
            CP("act" if g % 2 else "dve", xT[:, g * 4:(g + 1) * 4, tb * 128:(tb + 1) * 128],
               bank(b).rearrange("p (a c) -> p a c", a=4), [], [bname(b), f"xT{tb // 4}"])
    MS("pool", cin[:, 0:3], 0.0, ["cin"])

    setc = [0]
    evc = [0]

    def next_set():
        s_ = setc[0] % 2
        setc[0] += 1
        return s_

    def next_ev():
        k = evc[0] % 2
        evc[0] += 1
        return k

    set_banks = lambda st: [bname(st * 4 + j) for j in range(4)]
    set_ap = lambda st: PS[:, st * 2048:(st + 1) * 2048]

    for i, (name, c0, n, kind) in enumerate(SEGS):
        ws = i % 2
        CP("dve", wbf[ws][:, :, 0:n], wst[:, :, 0:n], ["wstA", "wstB"], [f"wbf{ws}"])
        if i + 1 < len(SEGS):
            load_w(i + 1)
        if kind == "FM":
            for cc in range(n // 128):
                st = next_set()
                for dc in range(16):
                    for tq in range(4):
                        MM(bank(st * 4 + tq), wbf[ws][:, dc, cc * 128:(cc + 1) * 128],
                           xT[:, dc, tq * 512:(tq + 1) * 512], dc == 0, dc == 15,
                           [f"wbf{ws}", f"xT{tq}"], st * 4 + tq)
                sb = set_banks(st)
                if name == "ckv":
                    CP("act", ckv_raw[:, cc, :], set_ap(st), [], sb + ["ckv_raw"])
                elif name in ("qi0", "qi1", "qa0", "qa1"):
                    k = next_ev()
                    idx = (0 if name[2] == "0" else 4) + cc
                    sc_ = 0.125 if name[1] == "i" else 128.0 ** -0.5
                    dst = (qiT_d if name[1] == "i" else qaT_d)[idx]
                    ACT(ev[k][:], set_ap(st), AF.Copy, [], sb + [f"ev{k}"], scale=sc_)
                    DMA("sp", dst, ev[k][:], [f"ev{k}"], [f"{name[:2]}T_d{idx}"], f"ev{k}")
                else:
                    ch = (0 if name == "qm" else 4) + cc
                    cw = lambda j: pp[:, PP_CONVW + ch * 4 + j: PP_CONVW + ch * 4 + j + 1]
                    CP("act", cin[:, 3:T + 3], set_ap(st), [], sb + ["cin"])
                    TS("dve", cacc[:], cin[:, 0:T], cw(0), pp[:, PP_CONVB + ch:PP_CONVB + ch + 1],
                       ALU.mult, ALU.add, ["cin", "pp"], ["cacc"])
                    for j in range(1, 4):
                        STT("dve", cacc[:], cin[:, j:T + j], cw(j), cacc[:], ALU.mult, ALU.add,
                            ["cin", "pp", "cacc"], ["cacc"])
                    k = next_ev()
                    ACT(ev[k][:], cacc[:], AF.Silu, ["cacc"], [f"ev{k}"])
                    DMA("sp", qkT_d[ch], ev[k][:], [f"ev{k}"], [f"qkT_d{ch}"], f"ev{k}")
        else:
            for tb4 in range(4):
                st = next_set()
                for j in range(4):
                    tb = tb4 * 4 + j
                    for dc in range(16):
                        MM(bank(st * 4 + j, n), xT[:, dc, tb * 128:(tb + 1) * 128], wbf[ws][:, dc, 0:n],
                           dc == 0, dc == 15, [f"wbf{ws}", f"xT{tb4}"], st * 4 + j)
                sb = set_banks(st)
                src = set_ap(st).rearrange("p (j c) -> p j c", j=4)[:, :, 0:n]
                if name == "kiw":
                    CP("dve", kw[:, tb4 * 4:(tb4 + 1) * 4, :], src, [], sb + ["kw"])
                elif name == "if":
                    CP("dve", ifg[:, tb4 * 4:(tb4 + 1) * 4, :], src, [], sb + ["ifg"])
                else:
                    k = next_ev()
                    func = {"za": AF.Silu, "vm": AF.Copy, "om": AF.Sigmoid, "zm": AF.Silu}[name[:2]]
                    dd_ = {"za": ga_d, "vm": vm_d, "om": gom_d, "zm": gzm_d}[name[:2]]
                    half = int(name[2])
                    evv = ev[k][:].rearrange("p (j c) -> p j c", j=4)
                    ACT(evv, src, func, [], sb + [f"ev{k}"])
                    dst = dd_.rearrange("(tb p) n -> p tb n", p=128)[:, tb4 * 4:(tb4 + 1) * 4,
                                                                     half * 512:(half + 1) * 512]
                    DMA("sp", dst, evv, [f"ev{k}"], [f"{name[:2]}_d"], f"ev{k}")
    P.barrier()
    if "ckv_raw" in dbg:
        dump("ckv_raw", ckv_raw[:], [128, 2, T], F32, ["ckv_raw"])
    if "kw" in dbg:
        dump("kw", kw[:], [128, 16, 80], F32, ["kw"])
        dump("ifg", ifg[:], [128, 16, 8], F32, ["ifg"])
    if upto <= 0:
        return finish(nc, P, dbg_out, out_d)

    a = Bump(mem, 5 * KB, 130 * KB, (1, 1))
    sq = a("sq", [128, 2, T], F32)
    rs = a("rs", [128, T], F32)
    wstg = a("wstg", [128, 2, 1024], F32)
    wstg2 = a("wstg2", [128, 2, 1024], F32)
    bcA = a("bcA", [128, 128], F32)
    m1 = a("m1", [128, 16, 1], F32)
    v1 = a("v1", [128, 16, 1], F32)
    cen = a("cen", [128, 16, 64], F32)
    sq2 = a("sq2", [128, 16, 64], F32)
    kdup = a("kdup", [128, 16, 128], F32)
    DMA("sp", wstg[:], wuk_d.rearrange("(j p) n -> p j n", p=128), [], ["wstg"], "wstg")
    DMA("pool", wstg2[:], wuv_d.rearrange("(j p) n -> p j n", p=128), [], ["wstg2"], "wstg2")
    DMA("sp", bcA[:], bc_d[:, BC_IDXG:BC_IDXG + 128], [], ["bcA"], "bcA")
    ACT(sq[:], ckv_raw[:], AF.Square, ["ckv_raw"], ["sq"])
    for tq in range(4):
        for j in range(2):
            MM(bank(tq), ones, sq[:, j, tq * 512:(tq + 1) * 512], j == 0, j == 1, ["cst", "sq"], tq)
    ACT(rs[:], PS[:, 0:2048], AF.Ln, ["epsT"], set_banks(0) + ["rs"], scale=1.0 / 256, bias=epsT[:])
    ACT(rs[:], rs[:], AF.Exp, ["rs"], ["rs"], scale=-0.5)
    for j in range(2):
        STT("dve", cT[:, j, :], ckv_raw[:, j, :], pp[:, PP_KVG + j:PP_KVG + j + 1], rs[:], ALU.mult, ALU.mult,
            ["ckv_raw", "pp", "rs"], ["cT"])
    CP("dve", wukb[:], wstg[:], ["wstg"], ["wukb"])
    CP("pool", wuvb[:], wstg2[:], ["wstg2"], ["wuvb"])
    ki = kw[:, :, 0:64]
    P.op("dve", lambda e: e.tensor_reduce(m1[:], ki, AX.X, ALU.add), reads=["kw"], writes=["m1"])
    TS("dve", m1[:], m1[:], -1.0 / 64, None, ALU.mult, None, ["m1"], ["m1"])
    TT("dve", cen[:], ki, m1[:].to_broadcast([128, 16, 64]), ALU.add, ["kw", "m1"], ["cen"])
    TT("dve", sq2[:], cen[:], cen[:], ALU.mult, ["cen"], ["sq2"])
    P.op("dve", lambda e: e.tensor_reduce(v1[:], sq2[:], AX.X, ALU.add), reads=["sq2"], writes=["v1"])
    ACT(v1[:], v1[:], AF.Ln, ["v1", "epsT"], ["v1"], scale=1.0 / 64, bias=epsT[:])
    ACT(v1[:], v1[:], AF.Exp, ["v1"], ["v1"], scale=-0.5)
    TT("dve", cen[:], cen[:], v1[:].to_broadcast([128, 16, 64]), ALU.mult, ["cen", "v1"], ["cen"])
    gview = bcA[:, 0:64].rearrange("p (o c) -> p o c", o=1).to_broadcast([128, 16, 64])
    bview = bcA[:, 64:128].rearrange("p (o c) -> p o c", o=1).to_broadcast([128, 16, 64])
    TT("dve", cen[:], cen[:], gview, ALU.mult, ["cen", "bcA"], ["cen"])
    TT("dve", kdup[:, :, 0:64], cen[:], bview, ALU.add, ["cen", "bcA"], ["kdup"])
    CP("pool", kdup[:, :, 64:128], kdup[:, :, 0:64], ["kdup"], ["kdup"])
    for tb in range(16):
        b = 4 + tb // 4
        TR(bank(b, 128, (tb % 4) * 128), kdup[:, tb, :], ident, ["kdup", "cst"], b)
        if tb % 4 == 3:
            CP("act", kiT2[:, (tb // 4) * 512:(tb // 4 + 1) * 512], bank(b), [], [bname(b), "kiT2"])
    TS("pool", wi[:], kw[:, :, 64:80], 0.25, None, ALU.mult, None, ["kw"], ["wi"])
    P.barrier()
    if "cT" in dbg:
        dump("cT", cT[:], [128, 2, T], BF16, ["cT"])
        dump("kiT2", kiT2[:], [128, T], BF16, ["kiT2"])
        dump("wi", wi[:], [128, 16, 16], F32, ["wi"])
    if upto <= 1:
        return finish(nc, P, dbg_out, out_d)

    s_ = Bump(mem, 5 * KB, 146 * KB, (2, 2))
    qs = [s_(f"qs{i}", [128, 16, 128], BF16) for i in range(2)]
    dg = [s_(f"dg{i}", [128, 16, 128], BF16) for i in range(2)]
    NR = 6
    Rb = [s_(f"R{i}", [128, 512], BF16) for i in range(NR)]
    scores2 = [s_(f"scores{i}", [128, 7424], F32) for i in range(2)]
    junk = [s_(f"junk{i}", [128, T], BF16) for i in range(3)]
    jc = [0]
    mtm = [s_(f"mtm{i}", [128, T], BF16) for i in range(2)]
    lo_t = s_(f"lo_t{g}", [128, 4], F32)
    w0_t = s_("w0_t", [128, 4], F32)
    mid_t = s_("mid_t", [128, 4], F32)
    cnt_t = s_("cnt_t", [128, 4], F32)
    ge_t = s_("ge_t", [128, 4], F32)
    hi_t = s_("hi_t", [128, 4], F32)
    qiT_r = qiT_d.rearrange("hp (r d) t -> d (hp r) t", r=2)
    MS("pool", maskT[:], 0.0, ["maskT"])
    for i in range(2):
        MS("dve", qs[i][64:128, :, :], 0.0, [f"qs{i}"])
    rc = [0]
    scb = [0]
    accb = [0]
    trb = [0]
    LAG = 2

    QORD = [4 * g + j for g in (3, 2, 1, 0) for j in range(4)]
    QPOS = {qb: i for i, qb in enumerate(QORD)}

    def diag_build(qb):
        sl = QPOS[qb] % 2
        DMA("sp", qs[sl][0:64, :, :], qiT_r[:, :, qb * 128:(qb + 1) * 128], [], [f"qs{sl}"], f"qs{sl}")
        for h in range(16):
            ACT(dg[sl][:, h, :], identb[:], AF.Copy, ["identb", "wi"], [f"dg{sl}_{h}"], scale=wi[:, qb, h:h + 1])

    lo_g = [s_(f"lo_g{i}", [128, 4], F32) for i in range(4)]

    def goffs(g):
        offs = {}
        o_ = 0
        for qb in range(4 * g, 4 * g + 4):
            offs[qb] = o_
            o_ += 128 * (qb + 1)
        return offs

    def indexer(g):
        scores = scores2[g % 2]
        offs = {}
        o_ = 0
        for qb in range(4 * g, 4 * g + 4):
            offs[qb] = o_
            o_ += 128 * (qb + 1)
        for qb in range(4 * g, 4 * g + 4):
            sl = QPOS[qb] % 2
            L = 128 * (qb + 1)
            if QPOS[qb] + 1 < 16:
                diag_build(QORD[QPOS[qb] + 1])
            for kc in range((L + 511) // 512):
                n = min(512, L - 512 * kc)
                ab = 4 + accb[0] % 2
                accb[0] += 1
                pend = []
                for step in range(16 + LAG):
                    if step < 16:
                        h = step
                        sb_ = scb[0] % 4
                        scb[0] += 1
                        MM(bank(sb_, n), qs[sl][:, h, :], kiT2[:, kc * 512:kc * 512 + n],
                           True, True, [f"qs{sl}", "kiT2"], sb_)
                        ri = rc[0] % NR
                        rc[0] += 1
                        if g == 3 and h % 2 == 1:
                            TS("dve", Rb[ri][:, 0:n], bank(sb_, n), 0.0, None, ALU.max, None, [],
                               [bname(sb_), f"R{ri}"])
                        else:
                            ACT(Rb[ri][:, 0:n], bank(sb_, n), AF.Relu, [], [bname(sb_), f"R{ri}"])
                        pend.append((h, ri))
                    if step >= LAG:
                        h, ri = pend[step - LAG]
                        MM(bank(ab, n), dg[sl][:, h, :], Rb[ri][:, 0:n], h == 0, h == 15, [f"dg{sl}_{h}", f"R{ri}"], ab)
                base = offs[qb] + kc * 512
                last = (kc == (L + 511) // 512 - 1)
                CP("act", scores[:, base:base + n], bank(ab, n), [], [bname(ab), f"sc{(qb // 4) % 2}_{qb % 4}"])
                if last:
                    TT("pool", scores[:, base + n - 128:base + n], scores[:, base + n - 128:base + n], cnegTM, ALU.add,
                       ["cst", f"sc{(qb // 4) % 2}_{qb % 4}"], [f"sc{(qb // 4) % 2}_{qb % 4}"])
    def bisect(g):
        scores = scores2[g % 2]
        offs = goffs(g)
        lo_t = lo_g[g]
        qbs = list(range(4 * g, 4 * g + 4))
        scr_names = [f"sc{(qb // 4) % 2}_{qb % 4}" for qb in qbs]
        MS("dve", lo_t[:], -1e29, [f"lo_t{g}"])
        MS("dve", w0_t[:], 0.0, ["w0_t"])
        MS("dve", cnt_t[:], 0.0, ["cnt_t"])
        for j, qb in enumerate(qbs):
            if qb < 2:
                continue
            L = 128 * (qb + 1)
            sv = scores[:, offs[qb]:offs[qb] + L]
            P.op("dve", lambda e, sv=sv, j=j: e.tensor_reduce(hi_t[:, j:j + 1], sv, AX.X, ALU.max),
                 reads=[f"sc{(qb // 4) % 2}_{qb % 4}"], writes=["hi_t"])
            P.op("dve", lambda e, sv=sv, j=j: e.tensor_reduce(lo_t[:, j:j + 1], sv[:, 0:256], AX.X, ALU.min),
                 reads=[f"sc{(qb // 4) % 2}_{qb % 4}"], writes=[f"lo_t{g}"])
            TT("dve", w0_t[:, j:j + 1], hi_t[:, j:j + 1], lo_t[:, j:j + 1], ALU.subtract, ["hi_t", f"lo_t{g}"], ["w0_t"])
        if g > 0 or True:
            for k in range(NIT):
                c_ = 0.5 ** (k + 1)
                STT("dve", mid_t[:], w0_t[:], c_, lo_t[:], ALU.mult, ALU.add, ["w0_t", f"lo_t{g}"], ["mid_t"])
                for j, qb in enumerate(qbs):
                    if qb < 2:
                        continue
                    L = 128 * (qb + 1)
                    ji = jc[0] % 3
                    jc[0] += 1
                    TS("dve", junk[ji][:, 0:L], scores[:, offs[qb]:offs[qb] + L], mid_t[:, j:j + 1], None,
                       ALU.is_ge, ALU.add, [f"sc{(qb // 4) % 2}_{qb % 4}", "mid_t"], [f"junk{ji}", f"cnt_t{j}"],
                       accum=cnt_t[:, j:j + 1])
                TS("dve", ge_t[:], cnt_t[:], float(TOPK), None, ALU.is_ge, None,
                   ["cnt_t"] + [f"cnt_t{j}" for j in range(4)], ["ge_t"])
                TT("dve", ge_t[:], ge_t[:], w0_t[:], ALU.mult, ["ge_t", "w0_t"], ["ge_t"])
                STT("dve", lo_t[:], ge_t[:], c_, lo_t[:], ALU.mult, ALU.add, ["ge_t", f"lo_t{g}"], [f"lo_t{g}"])
    def masks(g):
        scores = scores2[g % 2]
        offs = goffs(g)
        lo_t = lo_g[g]
        qbs = list(range(4 * g, 4 * g + 4))
        for j, qb in enumerate(qbs):
            L = 128 * (qb + 1)
            ms = qb % 2
            TS("dve", mtm[ms][:, 0:L], scores[:, offs[qb]:offs[qb] + L], lo_t[:, j:j + 1], None,
               ALU.is_ge, None, [f"sc{(qb // 4) % 2}_{qb % 4}", f"lo_t{g}"], [f"mtm{ms}"])
            for kb in range(qb + 1):
                tb_ = 6 + trb[0] % 2
                trb[0] += 1
                pt = bank(tb_).bitcast(BF16)[:, 0:128]
                TR(pt, mtm[ms][:, kb * 128:(kb + 1) * 128], identb[:], [f"mtm{ms}", "identb"], tb_)
                dst = maskT[:, mt_off[kb] + (qb - kb) * 128: mt_off[kb] + (qb - kb + 1) * 128]
                CP("act", dst, pt, ["maskT"], [bname(tb_), f"maskT_{kb}_{qb}"])
        if "thr" in dbg:
            dump(f"thr{g}", lo_t[:], [128, 4], F32, [f"lo_t{g}"])
    diag_build(QORD[0])
    indexer(3)
    bisect(3)
    for g in (2, 1, 0):
        indexer(g)
        masks(g + 1)
        bisect(g)
    masks(0)
    P.barrier()
    if "maskT" in dbg:
        dump("maskT", maskT[:], [128, MT_N], BF16, ["maskT"])
    if upto <= 2:
        return finish(nc, P, dbg_out, out_d)

    t_ = Bump(mem, 69 * KB, 156 * KB, (3, 3))
    Qh = [t_(f"Qh{i}", [128, T], BF16) for i in range(2)]
    KTh = [t_(f"KTh{i}", [128, T], BF16) for i in range(2)]
    Vh = [t_(f"Vh{i}", [128, 16, 129], BF16) for i in range(2)]
    gah = [t_(f"gah{i}", [128, 16, 128], BF16) for i in range(2)]
    NPT = 32
    PT = [t_(f"PT{i}", [128, 512], BF16) for i in range(NPT)]
    NE = 4
    Eb = [t_(f"E{i}", [128, 512], BF16) for i in range(NE)]
    bst = t_("bst", [128, 8, 2, 128], F32)
    bT = t_("bT", [128, 8, 2, 128], BF16)
    rden = [t_(f"rden{i}", [128, 1], F32) for i in range(2)]
    ya = [t_(f"ya{i}", [128, 128], BF16) for i in range(2)]
    DMA("sp", bst[:], bias_d.rearrange("h d p c -> p h d c"), [], ["bst"], "bst")
    for h in range(8):
        for dl in range(2):
            if dl == 0:
                STT("dve", bT[:, h, 0, :], bst[:, h, 0, :], pp[:, PP_B31 + h:PP_B31 + h + 1], cnegT,
                    ALU.subtract, ALU.add, ["bst", "pp", "cst"], ["bT"])
            else:
                TS("dve", bT[:, h, 1, :], bst[:, h, 1, :], pp[:, PP_B31 + h:PP_B31 + h + 1], None,
                   ALU.subtract, None, ["bst", "pp"], ["bT"])
    for i in range(2):
        MS("pool", Vh[i][:], 1.0, [f"Vh{i}"])
    ga_r = ga_d.rearrange("(tb p) n -> p tb n", p=128)
    ptc, ec, stc, pvc, yc = [0], [0], [0], [0], [0]
    pendq = []

    def attn_units(vres, Vt, vcols, weight_fn, post_fn, nst=4):
        def emit_pv(Tb, chunks):
            for j in range(4):
                qb = 4 * Tb + j
                pvb = 4 + pvc[0] % 2
                pvc[0] += 1
                for kb in range(qb + 1):
                    MM(bank(pvb, vcols), PT[chunks[kb]][:, j * 128:(j + 1) * 128], Vt[:, kb, :],
                       kb == 0, kb == qb, [f"PT{chunks[kb]}", vres], pvb)
                fin = post_fn(qb, pvb)
                if pendq:
                    pendq.pop(0)()
                pendq.append(fin)

        prev = None
        for Tb in range(4):
            nkb = 4 * Tb + 4
            chunks = {}
            for kb in range(nkb):
                c0 = max(Tb * 512, kb * 128)
                ncols = (Tb + 1) * 512 - c0
                lo_ = c0 - Tb * 512
                stb = stc[0] % nst
                stc[0] += 1
                pi = ptc[0] % NPT
                ptc[0] += 1
                chunks[kb] = pi
                weight_fn(kb, Tb, c0, ncols, lo_, stb, pi)
            if prev is not None:
                emit_pv(*prev)
            prev = (Tb, chunks)
        emit_pv(*prev)

    def prepT(h):
        hs = h % 2
        DMA("sp", Qh[hs][:], qaT_d[h], [], [f"Qh{hs}"], f"Qh{hs}")
        DMA("pool", gah[hs][:], ga_r[:, :, h * 128:(h + 1) * 128], [], [f"gah{hs}"], f"gah{hs}")
        for tq in range(4):
            for j in range(2):
                MM(bank(tq), wukb[:, j, h * 128:(h + 1) * 128], cT[:, j, tq * 512:(tq + 1) * 512],
                   j == 0, j == 1, ["wukb", "cT"], tq)
        CP("act", KTh[hs][:], PS[:, 0:2048], [], set_banks(0) + [f"KTh{hs}"])
        for tb4 in range(4):
            for jj in range(4):
                tb = tb4 * 4 + jj
                for j in range(2):
                    MM(bank(tb4, 128, jj * 128), cT[:, j, tb * 128:(tb + 1) * 128], wuvb[:, j, h * 128:(h + 1) * 128],
                       j == 0, j == 1, ["cT", "wuvb"], tb4)
            CP("dve", Vh[hs][:, tb4 * 4:(tb4 + 1) * 4, 0:128], bank(tb4).rearrange("p (a c) -> p a c", a=4),
               [], [bname(tb4), f"Vh{hs}"])

    prepT(0)
    for h in range(8):
        hs = h % 2
        if h + 1 < 8:
            prepT(h + 1)

        def wfn(kb, Tb, c0, ncols, lo_, stb, pi, h=h, hs=hs):
            dls = [dl for dl in (0, 1) if 4 * Tb <= kb + dl < 4 * Tb + 4]
            MM(bank(stb, ncols, lo_), KTh[hs][:, kb * 128:(kb + 1) * 128], Qh[hs][:, c0:c0 + ncols],
               True, len(dls) == 0, [f"KTh{hs}", f"Qh{hs}"], stb)
            for ii, dl in enumerate(dls):
                jj = kb + dl - 4 * Tb
                MM(bank(stb, 128, jj * 128), identb[:], bT[:, h, dl, :], False, ii == len(dls) - 1,
                   ["identb", "bT"], stb)
            ei = ec[0] % NE
            ec[0] += 1
            ACT(Eb[ei][:, lo_:512], bank(stb, ncols, lo_), AF.Exp, [], [bname(stb), f"E{ei}"])
            mo = mt_off[kb] + (c0 - 128 * kb)
            TT("dve", PT[pi][:, lo_:512], Eb[ei][:, lo_:512], maskT[:, mo:mo + ncols], ALU.mult,
               [f"E{ei}", "maskT"], [f"PT{pi}"])

        def pfn(qb, pvb, h=h, hs=hs):
            yi = yc[0] % 2
            yc[0] += 1
            P.op("dve", lambda e: e.reciprocal(rden[yi][:], bank(pvb, 1, 128)), writes=[bname(pvb), f"rden{yi}"])
            STT("dve", ya[yi][:], bank(pvb, 128), rden[yi][:], gah[hs][:, qb, :], ALU.mult, ALU.mult,
                [f"rden{yi}", f"gah{hs}"], [bname(pvb), f"ya{yi}"])
            def fin():
                tb_ = 6 + yi
                pt = bank(tb_).bitcast(BF16)[:, 0:128]
                TR(pt, ya[yi][:], identb[:], [f"ya{yi}", "identb"], tb_)
                CP("act", yT[:, h, qb * 128:(qb + 1) * 128], pt, [], [bname(tb_), "yT"])
            return fin

        attn_units(f"Vh{hs}", Vh[hs], 129, wfn, pfn)
    while pendq:
        pendq.pop(0)()
    P.barrier()
    if "yTa" in dbg:
        dump("yTa", yT[:, 0:8, :], [128, 8, T], BF16, ["yT"])
    if upto <= 3:
        return finish(nc, P, dbg_out, out_d)

    m_ = Bump(mem, 69 * KB, LIMIT, (4, 4))
    bcM = m_("bcM", [128, 1032], F32)
    fpre = m_("fpre", [128, 16, 4], F32)
    ipre = m_("ipre", [128, 16, 4], F32)
    lfn = m_("lfn", [128, 16, 4], F32)
    lfx = m_("lfx", [128, 16, 4], F32)
    u_t = m_("u_t", [128, 16, 4], F32)
    rt = [m_(f"rt{i}", [128, 128], F32) for i in range(4)]
    qmh = [m_(f"qmh{i}", [128, T], BF16) for i in range(2)]
    kmh = [m_(f"kmh{i}", [128, T], BF16) for i in range(2)]
    Vmh = [m_(f"Vmh{i}", [128, 16, 257], BF16) for i in range(2)]
    gomh = m_("gomh", [128, 16, 256], BF16)
    gzmh = m_("gzmh", [128, 16, 256], BF16)
    gall = [m_(f"gall{i}", [128, 16, 256], BF16) for i in range(2)]
    Fbc = [m_(f"Fbc{i}", [128, T], F32) for i in range(2)]
    NW = 4
    Wb = [m_(f"W{i}", [128, 512], F32) for i in range(NW)]
    PT = [m_(f"PTm{i}", [128, 512], BF16) for i in range(NPT)]
    NPB = 3
    nsb = [m_(f"nsb{i}", [128, 257], F32) for i in range(NPB)]
    hn = [m_(f"hn{i}", [128, 256], F32) for i in range(NPB)]
    ym = [m_(f"ym{i}", [128, 256], BF16) for i in range(NPB)]
    st6 = [m_(f"st6{i}", [128, 6], F32) for i in range(NPB)]
    mv = [m_(f"mv{i}", [128, 2], F32) for i in range(NPB)]
    ddt = [m_(f"dd{i}", [128, 1], F32) for i in range(NPB)]
    t1t = [m_(f"t1{i}", [128, 1], F32) for i in range(NPB)]
    DMA("sp", bcM[:], bc_d[:, BC_BI:BC_BI + 1032], [], ["bcM"], "bcM")
    for i in range(2):
        MS("pool", Vmh[i][:], 1.0, [f"Vmh{i}"])
    bi_v = bcM[:, 0:4].rearrange("p (o c) -> p o c", o=1).to_broadcast([128, 16, 4])
    bf_v = bcM[:, 4:8].rearrange("p (o c) -> p o c", o=1).to_broadcast([128, 16, 4])
    TT("dve", ipre[:], ifg[:, :, 0:4], bi_v, ALU.add, ["ifg", "bcM"], ["ipre"])
    TT("dve", fpre[:], ifg[:, :, 4:8], bf_v, ALU.add, ["ifg", "bcM"], ["fpre"])
    ACT(lfn[:], fpre[:], AF.Exp, ["fpre"], ["lfn"], scale=-1.0)
    ACT(lfn[:], lfn[:], AF.Ln, ["lfn"], ["lfn"], bias=1.0)
    MS("dve", lfx[:, 0, :], 0.0, ["lfx"])
    for tb in range(1, 16):
        TT("dve", lfx[:, tb, :], lfx[:, tb - 1, :], lfn[:, tb - 1, :], ALU.add, ["lfx", "lfn"], ["lfx"])
    MM(bank(0, 64), tri, lfn[:].rearrange("p a b -> p (a b)"), True, False, ["cst", "lfn"], 0)
    MM(bank(0, 64), ones, lfx[:].rearrange("p a b -> p (a b)"), False, True, ["cst", "lfx"], 0)
    STT("dve", u_t[:].rearrange("p a b -> p (a b)"), bank(0, 64), math.log(128.0 ** -0.5),
        ipre[:].rearrange("p a b -> p (a b)"), ALU.add, ALU.add, ["ipre"], [bname(0), "u_t"])
    vm_r = vm_d.rearrange("(tb p) n -> p tb n", p=128)
    gom_r = gom_d.rearrange("(tb p) n -> p tb n", p=128)
    gzm_r = gzm_d.rearrange("(tb p) n -> p tb n", p=128)
    wc, rtc = [0], [0]
    def prepM(h):
        hs = h % 2
        DMA("sp", qmh[hs][:], qkT_d[h], [], [f"qmh{hs}"], f"qmh{hs}")
        DMA("pool", kmh[hs][:], qkT_d[4 + h], [], [f"kmh{hs}"], f"kmh{hs}")
        DMA("sp", Vmh[hs][:, :, 0:256], vm_r[:, :, h * 256:(h + 1) * 256], [], [f"Vmh{hs}"], f"Vmh{hs}")
        DMA("pool", gomh[:], gom_r[:, :, h * 256:(h + 1) * 256], [], ["gomh"], "gomh")
        DMA("sp", gzmh[:], gzm_r[:, :, h * 256:(h + 1) * 256], [], ["gzmh"], "gzmh")
        TT("pool", gall[hs][:], gomh[:], gzmh[:], ALU.mult, ["gomh", "gzmh"], [f"gall{hs}"])
        mg = bcM[:, 8 + h * 256: 8 + (h + 1) * 256].rearrange("p (o c) -> p o c", o=1).to_broadcast([128, 16, 256])
        TT("pool", gall[hs][:], gall[hs][:], mg, ALU.mult, [f"gall{hs}", "bcM"], [f"gall{hs}"])
        for tb in range(16):
            ri = rtc[0] % 4
            rtc[0] += 1
            TS("dve", rt[ri][:], tri, lfn[:, tb, h:h + 1], lfx[:, tb, h:h + 1], ALU.mult, ALU.add,
               ["cst", "lfn", "lfx"], [f"rt{ri}"])
            b = tb // 4
            MM(bank(b, 128, (tb % 4) * 128), ones, rt[ri][:], True, True, ["cst", f"rt{ri}"], b)
            if tb % 4 == 3:
                ACT(Fbc[hs][:, b * 512:(b + 1) * 512], bank(b), AF.Copy, [], [bname(b), f"Fbc{hs}"], scale=-1.0)

    prepM(0)
    for h in range(4):
        hs = h % 2
        if h + 1 < 4:
            prepM(h + 1)

        def wfn(kb, Tb, c0, ncols, lo_, stb, pi, h=h, hs=hs):
            wi_ = wc[0] % NW
            wc[0] += 1
            ACT(Wb[wi_][:, lo_:512], Fbc[hs][:, c0:c0 + ncols], AF.Exp, [f"Fbc{hs}", "u_t"], [f"W{wi_}"],
                bias=u_t[:, kb, h:h + 1])
            MM(bank(stb, ncols, lo_), kmh[hs][:, kb * 128:(kb + 1) * 128], qmh[hs][:, c0:c0 + ncols],
               True, True, [f"kmh{hs}", f"qmh{hs}"], stb)
            TT("dve", PT[pi][:, lo_:512], bank(stb, ncols, lo_), Wb[wi_][:, lo_:512], ALU.mult,
               [f"W{wi_}"], [bname(stb), f"PT{pi}"])
            if kb >= 4 * Tb:
                TT("dve", PT[pi][:, lo_:lo_ + 128], PT[pi][:, lo_:lo_ + 128], c01Tb[:], ALU.mult,
                   [f"PT{pi}", "c01Tb"], [f"PT{pi}"])

        def pfn(qb, pvb, h=h, hs=hs):
            yi = yc[0] % NPB
            yc[0] += 1
            CP("act", nsb[yi][:], bank(pvb, 257), [], [bname(pvb), f"nsb{yi}"])
            Nap = nsb[yi][:, 0:256]
            nr = [f"nsb{yi}"]
            P.op("dve", lambda e: e.bn_stats(st6[yi][:], Nap), reads=nr, writes=[f"st6{yi}"])
            P.op("dve", lambda e: e.bn_aggr(mv[yi][:], st6[yi][:]), reads=[f"st6{yi}"], writes=[f"mv{yi}"])
            TT("dve", t1t[yi][:], nsb[yi][:, 256:257], nsb[yi][:, 256:257], ALU.mult, nr, [f"t1{yi}"])
            TS("dve", t1t[yi][:], t1t[yi][:], 1.0, LN_EPS, ALU.max, ALU.mult, [f"t1{yi}"], [f"t1{yi}"])
            TT("dve", t1t[yi][:], t1t[yi][:], mv[yi][:, 1:2], ALU.add, [f"t1{yi}", f"mv{yi}"], [f"t1{yi}"])
            ACT(t1t[yi][:], t1t[yi][:], AF.Ln, [f"t1{yi}"], [f"t1{yi}"])
            ACT(t1t[yi][:], t1t[yi][:], AF.Exp, [f"t1{yi}"], [f"t1{yi}"], scale=-0.5)
            TS("dve", hn[yi][:], Nap, mv[yi][:, 0:1], t1t[yi][:], ALU.subtract, ALU.mult,
               nr + [f"mv{yi}", f"t1{yi}"], [f"hn{yi}"])
            TT("dve", ym[yi][:], hn[yi][:], gall[hs][:, qb, :], ALU.mult, [f"hn{yi}", f"gall{hs}"], [f"ym{yi}"])
            def fin():
                tb_ = 6 + yi % 2
                pt = bank(tb_).bitcast(BF16)[:, 0:256]
                for i2 in range(2):
                    TR(pt[:, i2 * 128:(i2 + 1) * 128], ym[yi][:, i2 * 128:(i2 + 1) * 128], identb[:],
                       [f"ym{yi}", "identb"], tb_)
                CP("act", yT[:, 8 + 2 * h:10 + 2 * h, qb * 128:(qb + 1) * 128],
                   pt.rearrange("p (a c) -> p a c", a=2), [], [bname(tb_), "yT"])
            return fin

        attn_units(f"Vmh{hs}", Vmh[hs], 257, wfn, pfn)
    while pendq:
        pendq.pop(0)()
    P.barrier()
    if "yTm" in dbg:
        dump("yTm", yT[:, 8:16, :], [128, 8, T], BF16, ["yT"])
    if upto <= 4:
        return finish(nc, P, dbg_out, out_d)

    o_ = Bump(mem, 69 * KB, LIMIT, (5, 5))
    wo_bf = o_("wo_bf", [128, 16, D], BF16)
    wstO = [o_(f"wstO{i}", [128, 16, 256], F32) for i in range(2)]
    xres = [o_(f"xres{i}", [128, D], F32) for i in range(3)]
    lngb = o_("lngb", [128, 2 * D], F32)
    stO = [o_(f"stO{i}", [128, 4, 6], F32) for i in range(3)]
    mvO = [o_(f"mvO{i}", [128, 2], F32) for i in range(3)]
    rsO = [o_(f"rsO{i}", [128, 1], F32) for i in range(3)]
    wout_r = wout_d.rearrange("(fc p) n -> p fc n", p=128)
    DMA("pool", lngb[:], bc_d[:, BC_LNG:BC_LNG + 2 * D], [], ["lngb"], "lngb")
    for pc in range(8):
        s = pc % 2
        DMA("sp", wstO[s][:, 0:8, :], wout_r[:, 0:8, pc * 256:(pc + 1) * 256], [], [f"wstO{s}a"], f"wstO{s}a")
        DMA("pool", wstO[s][:, 8:16, :], wout_r[:, 8:16, pc * 256:(pc + 1) * 256], [], [f"wstO{s}b"], f"wstO{s}b")
        CP(("dve", "act")[pc % 2], wo_bf[:, :, pc * 256:(pc + 1) * 256], wstO[s][:],
           [f"wstO{s}a", f"wstO{s}b"], ["wo_bf"])
    for tb in range(16):
        s = tb % 3
        DMA("sp", xres[s][:], x_d[tb * 128:(tb + 1) * 128, :], [], [f"xres{s}"], f"xres{s}")
        st = next_set()
        for n4 in range(4):
            for fc in range(16):
                MM(bank(st * 4 + n4), yT[:, fc, tb * 128:(tb + 1) * 128], wo_bf[:, fc, n4 * 512:(n4 + 1) * 512],
                   fc == 0, fc == 15, ["yT", "wo_bf"], st * 4 + n4)
        STT("dve", xres[s][:], xres[s][:], ALPHA, set_ap(st), ALU.mult, ALU.add, [f"xres{s}"],
            set_banks(st) + [f"xres{s}"])
        for c4 in range(4):
            P.op("dve", lambda e, c4=c4, s=s: e.bn_stats(stO[s][:, c4, :], xres[s][:, c4 * 512:(c4 + 1) * 512]),
                 reads=[f"xres{s}"], writes=[f"stO{s}"])
        P.op("dve", lambda e, s=s: e.bn_aggr(mvO[s][:], stO[s][:].rearrange("p a b -> p (a b)")),
             reads=[f"stO{s}"], writes=[f"mvO{s}"])
        ACT(rsO[s][:], mvO[s][:, 1:2], AF.Ln, [f"mvO{s}", "epsT"], [f"rsO{s}"], bias=epsT[:])
        ACT(rsO[s][:], rsO[s][:], AF.Exp, [f"rsO{s}"], [f"rsO{s}"], scale=-0.5)
        TS("dve", xres[s][:], xres[s][:], mvO[s][:, 0:1], rsO[s][:], ALU.subtract, ALU.mult,
           [f"xres{s}", f"mvO{s}", f"rsO{s}"], [f"xres{s}"])
        TT("dve", xres[s][:], xres[s][:], lngb[:, 0:D], ALU.mult, [f"xres{s}", "lngb"], [f"xres{s}"])
        TT("pool", xres[s][:], xres[s][:], lngb[:, D:2 * D], ALU.add, [f"xres{s}", "lngb"], [f"xres{s}"])
        DMA("sp", out_d[tb * 128:(tb + 1) * 128, :], xres[s][:], [f"xres{s}"], [f"out{tb}"], f"xres{s}")
    return finish(nc, P, dbg_out, out_d)


def finish(nc, P, dbg_out, out_d):
    P.barrier()
    with ExitStack() as st:
        P.emit(st)
    return nc, dbg_out


def _t5_bucket(n):
    n = np.maximum(n, 0)
    nf = np.maximum(n, 1).astype(np.float32)
    large = 16 + (np.log(nf / 16) / math.log(128 / 16) * 16).astype(np.int32)
    large = np.minimum(large, 31)
    return np.where(n < 16, n, large)


def _host_inputs(inp):
    f = lambda a: np.ascontiguousarray(np.asarray(a, dtype=np.float32))
    w_ukT = f(np.asarray(inp["w_uk"])[0].transpose(2, 0, 1).reshape(256, 1024))
    w_uv2 = f(np.asarray(inp["w_uv"])[0].transpose(1, 0, 2).reshape(256, 1024))
    rel_bias = np.asarray(inp["rel_bias"], dtype=np.float32)
    p = np.arange(128)[:, None]
    c = np.arange(128)[None, :]
    biasT = np.zeros((8, 2, 128, 128), np.float32)
    for dl in range(2):
        nrel = 128 * dl + c - p
        g = rel_bias[_t5_bucket(nrel)]
        g = np.where((nrel >= 0)[..., None], g, np.float32(0))
        biasT[:, dl] = g.transpose(2, 0, 1)
    pp = np.zeros((128, PP_N), np.float32)
    pp[:, PP_KVG:PP_KVG + 2] = np.asarray(inp["kv_norm_g"])[0].reshape(2, 128).T
    cw = np.asarray(inp["conv_w"])[0]
    pp[:, PP_CONVW:PP_CONVW + 32] = cw.reshape(4, 8, 128).transpose(2, 1, 0).reshape(128, 32)
    pp[:, PP_CONVB:PP_CONVB + 8] = np.asarray(inp["conv_b"])[0].reshape(8, 128).T
    pp[:, PP_B31:PP_B31 + 8] = np.broadcast_to(rel_bias[31][None, :], (128, 8))
    bc = np.zeros((128, BC_N), np.float32)
    rep = lambda v: np.broadcast_to(np.asarray(v, dtype=np.float32).reshape(1, -1), (128, np.asarray(v).size))
    bc[:, BC_IDXG:BC_IDXG + 64] = rep(np.asarray(inp["idx_k_ln_g"])[0])
    bc[:, BC_IDXB:BC_IDXB + 64] = rep(np.asarray(inp["idx_k_ln_b"])[0])
    bc[:, BC_BI:BC_BI + 4] = rep(np.asarray(inp["b_igate"])[0])
    bc[:, BC_BF:BC_BF + 4] = rep(np.asarray(inp["b_fgate"])[0])
    bc[:, BC_MHG:BC_MHG + 1024] = rep(np.asarray(inp["mh_norm_g"])[0])
    bc[:, BC_LNG:BC_LNG + 2048] = rep(np.asarray(inp["ln_g"])[0])
    bc[:, BC_LNB:BC_LNB + 2048] = rep(np.asarray(inp["ln_b"])[0])
    cst = np.zeros((128, 5, 128), np.float32)
    cst[:, 0] = np.eye(128)
    cst[:, 1] = 1.0
    cst[:, 2] = (p <= c)
    cst[:, 3] = np.where(c <= p, 0.0, -1e30)
    cst[:, 4] = np.where(c >= p, 0.0, -30000.0)
    shared = dict(w_in=f(np.asarray(inp["w_in"])[0]), w_out=f(np.asarray(inp["w_out"])[0]),
                  w_ukT=w_ukT, w_uv2=w_uv2, biasT=biasT, pp=pp, bc=bc, cst=cst)
    x = np.asarray(inp["x"], dtype=np.float32)
    return [dict(shared, x=np.ascontiguousarray(x[i])) for i in range(8)]


def kernel(**inputs):
    nc, _ = build_program()
    in_maps = _host_inputs(inputs)
    res = run_bass_kernel_spmd(nc, in_maps, core_ids=list(range(8)))
    return np.stack([np.asarray(r["out"], dtype=np.float32) for r in res.results], axis=0)
```
